# Optimizing a Trainium2 kernel written in Bass

```python
import jax, jax.numpy as jnp
from jax import lax
import numpy as np

D_MODEL = 2048
BATCH = 4
SEQ = 2048
DEPTH = 1

CHUNK = 64
Q_BLOCK = 128
SB_HEADS = 8
SB_HEAD_DIM = 128
SB_WIDTH = SB_HEADS * SB_HEAD_DIM
MLA_HEADS = 8
MLA_NOPE_DIM = 128
MLA_ROPE_DIM = 64
MLA_V_DIM = 128
MLA_Q_RANK = 512
MLA_KV_RANK = 256
MLA_QK_DIM = MLA_NOPE_DIM + MLA_ROPE_DIM
MLA_WIDTH = MLA_HEADS * MLA_V_DIM
MIX_WIDTH = SB_WIDTH + MLA_WIDTH
IN_COLS = 3 * SB_WIDTH + MLA_Q_RANK + MLA_KV_RANK + MLA_ROPE_DIM
D_FF = 5632
CONV_WIDTH = 3
ROPE_THETA = 10000.0
EPS = 1e-6

kernel_name = "hymba_stickbreak_mla_convffn_block"


def rmsnorm(x, g):
    xf = x.astype(jnp.float32)
    y = xf * lax.rsqrt(jnp.mean(xf * xf, axis=-1, keepdims=True) + EPS)
    return (y * g.astype(jnp.float32)).astype(x.dtype)


def rope_tables(positions):
    half = MLA_ROPE_DIM // 2
    inv_freq = ROPE_THETA ** (-jnp.arange(half, dtype=jnp.float32) / half)
    ang = positions.astype(jnp.float32)[..., None] * inv_freq
    return jnp.cos(ang), jnp.sin(ang)


def apply_rope(x, cos, sin):
    x1, x2 = jnp.split(x.astype(jnp.float32), 2, axis=-1)
    return jnp.concatenate([x1 * cos - x2 * sin, x1 * sin + x2 * cos], axis=-1).astype(x.dtype)


def stick_breaking_attention(q, k, v):
    S, Dh = q.shape[1], q.shape[-1]
    scale = Dh ** -0.5
    outs = []
    for q0 in range(0, S, Q_BLOCK):
        q_end = q0 + Q_BLOCK
        z = jnp.einsum('bqhd,bkhd->bhqk', q[:, q0:q_end], k[:, :q_end]).astype(jnp.float32) * scale
        t = q0 + jnp.arange(Q_BLOCK)
        s = jnp.arange(q_end)
        mask = s[None, :] < t[:, None]
        log_1m = jnp.where(mask, -jax.nn.softplus(z), 0.0)
        after = lax.cumsum(log_1m, axis=3, reverse=True) - log_1m
        a = jnp.where(mask, jnp.exp(jax.nn.log_sigmoid(z) + after), 0.0)
        outs.append(jnp.einsum('bhqk,bkhd->bqhd', a.astype(v.dtype), v[:, :q_end]))
    return jnp.concatenate(outs, axis=1)


def mla_attention(q_nope, q_rope, k_nope, k_rope, v):
    S = q_nope.shape[1]
    scale = MLA_QK_DIM ** -0.5
    outs = []
    for q0 in range(0, S, Q_BLOCK):
        q_end = q0 + Q_BLOCK
        s_nope = jnp.einsum('bqhd,bkhd->bhqk', q_nope[:, q0:q_end], k_nope[:, :q_end])
        s_rope = jnp.einsum('bqhd,bkd->bhqk', q_rope[:, q0:q_end], k_rope[:, :q_end])
        scores = (s_nope.astype(jnp.float32) + s_rope.astype(jnp.float32)) * scale
        t_chunk = (q0 + jnp.arange(Q_BLOCK)) // CHUNK
        s_chunk = jnp.arange(q_end) // CHUNK
        mask = s_chunk[None, :] <= t_chunk[:, None]
        p = jax.nn.softmax(jnp.where(mask, scores, -jnp.inf), axis=-1)
        outs.append(jnp.einsum('bhqk,bkhd->bqhd', p.astype(v.dtype), v[:, :q_end]))
    return jnp.concatenate(outs, axis=1)


def causal_depthwise_conv(u, w, b):
    S = u.shape[1]
    up = jnp.pad(u, ((0, 0), (CONV_WIDTH - 1, 0), (0, 0)))
    y = up[:, 0:S] * w[0]
    for j in range(1, CONV_WIDTH):
        y = y + up[:, j:j + S] * w[j]
    return y + b


def setup_inputs(seed: int = 0) -> dict:
    key = jax.random.key(seed)
    ks = jax.random.split(key, 20)
    f32 = jnp.float32
    L = DEPTH

    def wt(k, shape, fan_in):
        return jax.random.normal(k, shape, f32) * fan_in ** -0.5

    def gain(k, shape):
        return 1.0 + 0.02 * jax.random.normal(k, shape, f32)

    x = jax.random.normal(ks[0], (BATCH, SEQ, D_MODEL), f32)
    positions = jnp.tile(jnp.arange(SEQ, dtype=jnp.int32)[None, :], (BATCH, 1))
    return {
        "x": x,
        "positions": positions,
        "g_attn_pre": gain(ks[1], (L, D_MODEL)),
        "w_in": wt(ks[2], (L, D_MODEL, IN_COLS), D_MODEL),
        "g_cq": gain(ks[3], (L, MLA_Q_RANK)),
        "w_uq": wt(ks[4], (L, MLA_Q_RANK, MLA_HEADS * MLA_QK_DIM), MLA_Q_RANK),
        "g_ckv": gain(ks[5], (L, MLA_KV_RANK)),
        "w_ukv": wt(ks[6], (L, MLA_KV_RANK, MLA_HEADS * (MLA_NOPE_DIM + MLA_V_DIM)), MLA_KV_RANK),
        "g_out_sb": gain(ks[7], (L, SB_WIDTH)),
        "g_out_mla": gain(ks[8], (L, MLA_WIDTH)),
        "w_o": wt(ks[9], (L, MIX_WIDTH, D_MODEL), MIX_WIDTH),
        "g_attn_post": gain(ks[10], (L, D_MODEL)),
        "g_ffn_pre": gain(ks[11], (L, D_MODEL)),
        "w_up": wt(ks[12], (L, D_MODEL, 2 * D_FF), D_MODEL),
        "conv_w": wt(ks[13], (L, CONV_WIDTH, 2 * D_FF), CONV_WIDTH),
        "conv_b": 0.02 * jax.random.normal(ks[14], (L, 2 * D_FF), f32),
        "w_down": wt(ks[15], (L, D_FF, D_MODEL), D_FF),
        "g_ffn_post": gain(ks[16], (L, D_MODEL)),
    }


def reference(x, positions, g_attn_pre, w_in, g_cq, w_uq, g_ckv, w_ukv, g_out_sb, g_out_mla,
              w_o, g_attn_post, g_ffn_pre, w_up, conv_w, conv_b, w_down, g_ffn_post):
    B, S, _ = x.shape
    cos, sin = rope_tables(positions)
    split_points = np.cumsum([SB_WIDTH, SB_WIDTH, SB_WIDTH, MLA_Q_RANK, MLA_KV_RANK])
    split_points = split_points.tolist()
    for l in range(DEPTH):
        h = rmsnorm(x, g_attn_pre[l])
        proj = h @ w_in[l]
        q_sb, k_sb, v_sb, c_q, c_kv, k_rope = jnp.split(proj, split_points, axis=-1)

        o_sb = stick_breaking_attention(
            q_sb.reshape(B, S, SB_HEADS, SB_HEAD_DIM),
            k_sb.reshape(B, S, SB_HEADS, SB_HEAD_DIM),
            v_sb.reshape(B, S, SB_HEADS, SB_HEAD_DIM)).reshape(B, S, SB_WIDTH)

        q = (rmsnorm(c_q, g_cq[l]) @ w_uq[l]).reshape(B, S, MLA_HEADS, MLA_QK_DIM)
        q_nope, q_rope = jnp.split(q, [MLA_NOPE_DIM], axis=-1)
        q_rope = apply_rope(q_rope, cos[:, :, None, :], sin[:, :, None, :])
        kv = (rmsnorm(c_kv, g_ckv[l]) @ w_ukv[l]).reshape(B, S, MLA_HEADS, MLA_NOPE_DIM + MLA_V_DIM)
        k_nope, v_mla = jnp.split(kv, [MLA_NOPE_DIM], axis=-1)
        k_rope = apply_rope(k_rope, cos, sin)
        o_mla = mla_attention(q_nope, q_rope, k_nope, k_rope, v_mla).reshape(B, S, MLA_WIDTH)

        mixed = jnp.concatenate([rmsnorm(o_sb, g_out_sb[l]), rmsnorm(o_mla, g_out_mla[l])], axis=-1)
        x = x + rmsnorm(mixed @ w_o[l], g_attn_post[l])

        h = rmsnorm(x, g_ffn_pre[l])
        u = causal_depthwise_conv(h @ w_up[l], conv_w[l], conv_b[l])
        gate, val = jnp.split(u, 2, axis=-1)
        y = (jax.nn.gelu(gate, approximate=True) * val) @ w_down[l]
        x = x + rmsnorm(y, g_ffn_post[l])
    return x
```

```python
import math
import numpy as np
import ml_dtypes
import concourse.bass as bass
import concourse.mybir as mybir
from concourse.bass_utils import run_bass_kernel_spmd

F32 = mybir.dt.float32
BF16 = mybir.dt.bfloat16
I32 = mybir.dt.int32
AF = mybir.ActivationFunctionType
ALU = mybir.AluOpType
PI = math.pi

D = 2048
S = 2048
NB = 4
CH = 258
MW = 272
CHP = 272
DFF = 5632
NCT = 44
EPS = 1e-6
NEG = -30000.0
CHUNKS = ([0, 2, 5, 7], [1, 3, 4, 6])
NKB = [4, 8, 12, 16]

GCQ, GCKV, GOSB, GOMLA, GFP, CB, CW0, CW1, CW2, INVF, SGN, NG = 0, 4, 6, 14, 22, 38, 126, 214, 302, 390, 391, 392

DEBUG = {}


class Buf:
    __slots__ = ("w", "r", "excl", "acc")

    def __init__(self, excl=False):
        self.w = None
        self.r = {}
        self.excl = excl
        self.acc = {}


class Chan:
    __slots__ = ("sem", "count", "last")

    def __init__(self, sem):
        self.sem = sem
        self.count = 0
        self.last = None


class Prog:
    def __init__(self, nc):
        self.nc = nc
        self.eng = {"pe": nc.tensor, "act": nc.scalar, "dve": nc.vector, "pool": nc.gpsimd, "sp": nc.sync}
        self.sem = {}
        self.cnt = {}
        self.seen = {}
        for e in self.eng:
            self.sem[e] = nc.alloc_semaphore("prog_" + e)
            self.cnt[e] = 0
            self.seen[e] = {}
        self.chans = []
        self.nops = 0
        self.nwaits = 0

    def chan(self, name):
        c = Chan(self.nc.alloc_semaphore("ch_" + name))
        self.chans.append(c)
        return c

    def _wait_need(self, e, need):
        seen = self.seen[e]
        for sem, val in need.items():
            if seen.get(sem, 0) >= val:
                continue
            self.eng[e].wait_ge(sem, val)
            seen[sem] = val
            self.nwaits += 1

    def _collect(self, e, reads, writes, extra=()):
        need = {}
        pe_sem = self.sem["pe"]

        def add(sem, val):
            if e == "pe" and sem is pe_sem:
                return
            if need.get(sem, 0) < val:
                need[sem] = val

        own = self.sem.get(e)
        for b in reads:
            if b.excl:
                for sem, val in b.acc.items():
                    if sem is not own:
                        add(sem, val)
            elif b.w is not None:
                add(*b.w)
        for b in writes:
            if b.excl:
                for sem, val in b.acc.items():
                    if sem is not own:
                        add(sem, val)
                continue
            if b.w is not None:
                add(*b.w)
            for sem, val in b.r.items():
                add(sem, val)
        for t in extra:
            if t is not None:
                add(*t)
        self._wait_need(e, need)

    def _record(self, tok, reads, writes):
        sem, val = tok
        for b in reads:
            if b.excl:
                b.acc[sem] = val
            elif b.r.get(sem, 0) < val:
                b.r[sem] = val
        for b in writes:
            if b.excl:
                b.acc[sem] = val
                continue
            b.w = tok
            b.r = {}

    def op(self, e, fn, reads=(), writes=()):
        self._collect(e, reads, writes)
        ins = fn(self.eng[e])
        self.cnt[e] += 1
        tok = (self.sem[e], self.cnt[e])
        ins.then_inc(self.sem[e], 1)
        self._record(tok, reads, writes)
        self.nops += 1
        return tok

    def dma(self, q, out, in_, reads, writes, chan):
        self._collect(q, reads, writes, extra=(chan.last,))
        ins = self.eng[q].dma_start(out=out, in_=in_)
        chan.count += 1
        tok = (chan.sem, 16 * chan.count)
        ins.then_inc(chan.sem, 16)
        chan.last = tok
        self._record(tok, reads, writes)
        self.nops += 1
        return tok

    def barrier(self):
        need = {}
        for e in self.eng:
            if self.cnt[e] > 0:
                need[self.sem[e]] = self.cnt[e]
        for c in self.chans:
            if c.last is not None:
                need[c.last[0]] = c.last[1]
        for e in self.eng:
            self._wait_need(e, dict(need))

    def finish(self, toks):
        need = {}
        for t in toks:
            if t is None:
                continue
            if need.get(t[0], 0) < t[1]:
                need[t[0]] = t[1]
        self._wait_need("sp", need)


class Arena:
    def __init__(self, nc, lo, hi):
        self.nc = nc
        self.free = [(lo, hi)]
        self.live = {}
        self.uid = 0

    def alloc(self, name, shape, dtype, top=False):
        bpe = 4 if dtype in (F32, I32) else 2
        n = 1
        for d in shape[1:]:
            n *= d
        size = (n * bpe + 63) // 64 * 64
        order = range(len(self.free) - 1, -1, -1) if top else range(len(self.free))
        for i in order:
            a, b = self.free[i]
            if b - a >= size:
                if top:
                    self.free[i] = (a, b - size)
                    a = b - size
                else:
                    self.free[i] = (a + size, b)
                self.uid += 1
                h = self.nc.alloc_sbuf_tensor_at(f"{name}_{self.uid}", list(shape), dtype, offset=a)
                self.live[name] = (a, size)
                return h
        raise RuntimeError(f"arena OOM for {name} size {size}; free={self.free}")

    def release(self, *names):
        for name in names:
            a, size = self.live.pop(name)
            self.free.append((a, a + size))
        self.free.sort()
        merged = []
        for a, b in self.free:
            if merged and merged[-1][1] == a:
                merged[-1] = (merged[-1][0], b)
            else:
                merged.append((a, b))
        self.free = [(a, b) for a, b in merged if b > a]


def build_program(stop=None):
    nc = bass.Bass("TRN2", target_bir_lowering=False)
    P = Prog(nc)

    def stop_here(tag):
        if stop == tag:
            P.barrier()
            P.finish(list(dbg_outs.values()))
            print("STOP at", tag, "ops", P.nops, "waits", P.nwaits)
            return True
        return False

    def din(name, shape, dtype=F32):
        return nc.dram_tensor(name, list(shape), dtype, kind="ExternalInput")

    x_ctx = din("x_ctx", [S, D])
    x_own = din("x_own", [1024, D])
    x_halo = din("x_halo", [128, D])
    pos_ctx = din("pos_ctx", [1, S], I32)
    pos_own = din("pos_own", [1, 4 * CHP], I32)
    w_in = din("w_in", [D, 3904])
    w_uq = din("w_uq", [512, 1536])
    w_ukv = din("w_ukv", [256, 2048])
    w_o = din("w_o", [D, D])
    w_up = din("w_up", [D, 2 * DFF])
    w_down = din("w_down", [DFF, D])
    gp_d = din("gp", [128, NG])
    grow = din("grow", [3, D])
    ident_d = din("ident", [128, 128])
    cbf_d = din("cbf", [128, 4, 128], BF16)
    msb_d = din("msb", [128, 19, MW], BF16)
    mml_d = din("mml", [128, 19, MW], BF16)
    hf_d = din("hflag", [128, 8])
    out_d = nc.dram_tensor("out", [1024, D], F32, kind="ExternalOutput")
    x1s_d = nc.dram_tensor("x1s", [1024, D], F32)
    dbg_outs = {}

    A = nc.alloc_sbuf_tensor
    ident = A("ident_s", [128, 128], F32)
    cbf = A("cbf_s", [128, 4, 128], BF16)
    gp = A("gp_s", [128, NG], F32)
    eps_t = A("eps_t", [128, 1], F32)
    ssA = A("ssA", [128, 128], F32)
    rsA = A("rsA", [128, 64], F32)
    hf = A("hf_s", [128, 8], F32)
    B_const = Buf()
    chc = P.chan("const")
    P.dma("sp", ident[:], ident_d.ap(), [], [B_const], chc)
    P.dma("sp", cbf[:], cbf_d.ap(), [], [B_const], chc)
    P.dma("sp", gp[:], gp_d.ap(), [], [B_const], chc)
    P.dma("sp", hf[:], hf_d.ap(), [], [B_const], chc)
    P.op("pool", lambda e: e.memset(eps_t[:], EPS), [], [B_const])
    P.op("pool", lambda e: e.memset(ssA[:], 0.0), [], [B_const])
    P.op("pool", lambda e: e.memset(rsA[:], 0.0), [], [B_const])
    IDB, ONES, NTRI, OROW = 0, 1, 2, 3
    P.barrier()

    psA = nc.alloc_psum_tensor("psA", [128, 4, 512], F32)
    psB = nc.alloc_psum_tensor("psB", [128, 4, 512], F32)
    PB = [Buf(excl=True) for _ in range(8)]

    def PST(i):
        return psA if i < 4 else psB

    def PS(i, lo, hi, p0=0, p1=128):
        return PST(i)[p0:p1, i % 4, lo:hi]

    def PSV(i, dims, off=0):
        return bass.AP(PST(i), (i % 4) * 512 + off, [[2048, 128]] + dims)

    dbg_buf = Buf()
    dbg_ch = []
    if DEBUG:
        dbg_tmp = A("dbg_tmp", [128, 264], F32)
        dbg_ch.append(P.chan("dbg"))
    import os
    WB = [int(v) for v in os.environ.get("WARM_B", "1,0,0").split(",")]
    WC = [int(v) for v in os.environ.get("WARM_C", "0,0").split(",")]
    WD = int(os.environ.get("WARM_D", "0"))

    def pe_warm(n, src_ap):
        if n <= 0:
            return

        def mm(e):
            r = None
            for _ in range(n):
                r = e.matmul(PS(7, 0, 512), cbf[:, ONES, :], src_ap, start=True, stop=True)
            return r
        P.op("pe", mm, [], [PB[7]])

    lo = (nc.sbuf_base + 63) // 64 * 64
    AR = Arena(nc, lo, nc.sbuf_top)

    def rsqrt_ops(dst, src, scale, rd, wr):
        P.op("act", lambda e: e.activation(dst, src, AF.Ln, scale=scale, bias=eps_t[:]), rd + [B_const], wr)
        P.op("act", lambda e: e.activation(dst, dst, AF.Exp, scale=-0.5), wr, wr)

    def dump(name, ap_sb, shape, rd):
        if name not in DEBUG:
            return
        t = nc.dram_tensor("dbg_" + name, list(shape), F32, kind="ExternalOutput")
        tv = dbg_tmp[0:shape[0], 0:shape[1]]
        P.op("dve", lambda e: e.tensor_copy(tv, ap_sb), rd, [dbg_buf])
        dbg_outs[name] = P.dma("sp", t.ap(), tv, [dbg_buf], [], dbg_ch[0])

    xT_ctx = AR.alloc("xT_ctx", [128, 16, S], BF16)
    xT_own = AR.alloc("xT_own", [128, 16, 4, CHP], BF16)
    B_xTc = [[Buf() for _ in range(4)] for _ in range(16)]
    B_xTo = [[Buf() for _ in range(4)] for _ in range(16)]
    xa = [AR.alloc(f"xa{i}", [128, D], F32) for i in range(4)]
    B_xa = [Buf() for _ in range(4)]
    ch_xa = [P.chan(f"xa{i}") for i in range(4)]
    xbt = [AR.alloc(f"xbt{i}", [128, D], BF16) for i in range(8)]
    B_xb = [Buf() for _ in range(8)]

    def PSB(i):
        return PST(i)[:, i % 4, :].bitcast(BF16)
    junkA = AR.alloc("junkA", [128, D], BF16)
    B_junk = Buf()
    gpre_b = AR.alloc("gpre_b", [128, D], F32)
    B_gb = Buf()
    chg = P.chan("gb")
    P.dma("sp", gpre_b[:], bass.AP(grow, 0, [[0, 128], [1, D]]), [], [B_gb], chg)
    B_ss = [Buf() for _ in range(64)]
    groups = [("c", g) for g in range(4)] + [("o", g) for g in range(2)] + [("h", 0)]
    blkc = [0]
    evc = [0]

    def a1_stage1(gi):
        kind, g = groups[gi]
        nb = 1 if kind == "h" else 4
        for i in range(nb):
            xi = i
            bi_ = 4 * (gi % 2) + i
            blk = blkc[0]
            blkc[0] += 1
            if kind == "c":
                src = x_ctx.ap()[(4 * g + i) * 128:(4 * g + i + 1) * 128, :]
            elif kind == "o":
                src = x_own.ap()[(4 * g + i) * 128:(4 * g + i + 1) * 128, :]
            else:
                src = x_halo.ap()
            P.dma("sp", xa[xi][:], src, [], [B_xa[xi]], ch_xa[xi])
            P.op("act", lambda e: e.activation(junkA[:], xa[xi][:], AF.Square, accum_out=ssA[:, blk:blk + 1]),
                 [B_xa[xi]], [B_junk, B_ss[blk]])
            rsqrt_ops(rsA[:, blk:blk + 1], ssA[:, blk:blk + 1], 1.0 / D, [B_ss[blk]], [B_ss[blk]])
            P.op("dve", lambda e: e.scalar_tensor_tensor(xbt[bi_][:], xa[xi][:], rsA[:, blk:blk + 1], gpre_b[:], ALU.mult, ALU.mult),
                 [B_xa[xi], B_ss[blk], B_gb], [B_xb[bi_]])

    def a1_stage2(gi):
        kind, g = groups[gi]
        nb = 1 if kind == "h" else 4
        xs_ = [xbt[4 * (gi % 2) + i] for i in range(nb)]
        Bx = [B_xb[4 * (gi % 2) + i] for i in range(nb)]
        for k in range(16):
            bi = k % 8

            def tr(e):
                r = None
                for i in range(nb):
                    r = e.transpose(PSB(bi)[:, i * 128:(i + 1) * 128], xs_[i][:, k * 128:(k + 1) * 128], cbf[:, IDB, :])
                return r
            P.op("pe", tr, Bx + [B_const], [PB[bi]])
            eng = "act" if evc[0] % 2 == 0 else "dve"
            evc[0] += 1
            if kind == "c":
                o_ap, i_ap, wr = xT_ctx[:, k, g * 512:(g + 1) * 512], PSB(bi)[:, 0:512], [B_xTc[k][g]]
            elif kind == "o":
                o_ap = xT_own[:, k, 2 * g:2 * g + 2, 2:CH]
                i_ap = PSB(bi)[:, 0:512].rearrange("p (a b) -> p a b", b=256)
                wr = [B_xTo[k][2 * g], B_xTo[k][2 * g + 1]]
            else:
                o_ap = xT_own[:, k, :, 0:2]
                i_ap = PSB(bi)[:, 0:128].rearrange("p (s j) -> p s j", j=32)[:, :, 0:2]
                wr = [B_xTo[k][s] for s in range(4)]
            if eng == "act":
                P.op("act", lambda e: e.activation(o_ap, i_ap, AF.Copy), [PB[bi]], wr)
            else:
                P.op("dve", lambda e: e.tensor_copy(o_ap, i_ap), [PB[bi]], wr)

    a1_stage1(0)
    for gi in range(len(groups)):
        if gi + 1 < len(groups):
            a1_stage1(gi + 1)
        a1_stage2(gi)
    P.barrier()
    AR.release("xa0", "xa1", "xa2", "xa3", "xbt0", "xbt1", "xbt2", "xbt3", "xbt4", "xbt5", "xbt6", "xbt7", "junkA", "gpre_b")

    if stop_here("A1"):
        return nc
    kT_sb = AR.alloc("kT_sb", [128, 8, S], BF16)
    v_sb = AR.alloc("v_sb", [128, 16, 1024], BF16)
    cqnT = AR.alloc("cqnT", [128, 4, 4, CHP], BF16)
    ckvnT = AR.alloc("ckvnT", [128, 2, S], BF16)
    krT = AR.alloc("krT", [64, S], BF16)
    B_big = Buf()
    wkr = AR.alloc("wkr", [128, 16, 64], BF16)
    wkrs = AR.alloc("wkrs", [128, 16, 64], BF16)
    B_wkr = Buf()
    ch_wkr = P.chan("wkr")
    w_in_v = w_in.ap().rearrange("(k p) c -> p k c", p=128)
    P.dma("pool", wkr[:], w_in_v[:, :, 3840:3904], [], [B_wkr], ch_wkr)
    P.dma("pool", wkrs[:, :, 0:32], w_in_v[:, :, 3872:3904], [], [B_wkr], ch_wkr)
    P.dma("pool", wkrs[:, :, 32:64], w_in_v[:, :, 3840:3872], [], [B_wkr], ch_wkr)

    def make_tables(pos_ap_bcast, ncols, CC, SS, tmpf, tmpi, B_t, chp):
        posi, ang, a2, tf = tmpi, tmpf[0], tmpf[1], tmpf[2]
        P.dma("sp", posi[:, 0:ncols], pos_ap_bcast, [], [B_t], chp)
        P.op("dve", lambda e: e.tensor_copy(ang[:, 0:ncols], posi[:, 0:ncols]), [B_t], [B_t])
        P.op("dve", lambda e: e.tensor_scalar(ang[:, 0:ncols], ang[:, 0:ncols], gp[0:64, INVF:INVF + 1], None, ALU.mult),
             [B_t, B_const], [B_t])
        for shift, dst, sg in ((PI / 2, CC, False), (0.0, SS, True)):
            a2v, tfv, tiv = a2[:, 0:ncols], tf[:, 0:ncols], posi[:, 0:ncols]
            P.op("dve", lambda e: e.tensor_scalar(a2v, ang[:, 0:ncols], shift, None, ALU.add), [B_t], [B_t])
            P.op("dve", lambda e: e.tensor_scalar(tfv, a2v, 1.0 / (2 * PI), None, ALU.mult), [B_t], [B_t])
            P.op("dve", lambda e: e.tensor_copy(tiv, tfv), [B_t], [B_t])
            P.op("dve", lambda e: e.tensor_copy(tfv, tiv), [B_t], [B_t])
            P.op("dve", lambda e: e.scalar_tensor_tensor(a2v, tfv, -2 * PI, a2v, ALU.mult, ALU.add), [B_t], [B_t])
            P.op("dve", lambda e: e.tensor_single_scalar(tfv, a2v, PI, ALU.is_gt), [B_t], [B_t])
            P.op("dve", lambda e: e.scalar_tensor_tensor(a2v, tfv, -2 * PI, a2v, ALU.mult, ALU.add), [B_t], [B_t])
            P.op("dve", lambda e: e.tensor_single_scalar(tfv, a2v, -PI, ALU.is_lt), [B_t], [B_t])
            P.op("dve", lambda e: e.scalar_tensor_tensor(a2v, tfv, 2 * PI, a2v, ALU.mult, ALU.add), [B_t], [B_t])
            P.op("act", lambda e: e.activation(dst, a2v, AF.Sin), [B_t], [B_t])
            if sg:
                P.op("dve", lambda e: e.tensor_scalar(dst, dst, gp[0:64, SGN:SGN + 1], None, ALU.mult), [B_t, B_const], [B_t])

    tmpf = [AR.alloc(f"tbf{i}", [64, 512], F32) for i in range(3)]
    tmpi = AR.alloc("tbi", [64, 512], I32)
    CCc = AR.alloc("CCc", [64, 512], F32)
    SSc = AR.alloc("SSc", [64, 512], F32)
    t1k = AR.alloc("t1k", [64, 512], F32)
    t2k = AR.alloc("t2k", [64, 512], F32)
    B_tab = Buf()
    B_t12 = Buf()
    chp = P.chan("pos")
    for n in range(4):
        make_tables(bass.AP(pos_ctx, n * 512, [[0, 64], [1, 512]]), 512, CCc[:], SSc[:], tmpf, tmpi, B_tab, chp)
        b1, b2 = 6, 7
        for bb, wsel in ((b1, wkr), (b2, wkrs)):
            def mm(e):
                r = None
                for k in range(16):
                    r = e.matmul(PS(bb, 0, 512, 0, 64), wsel[:, k, :], xT_ctx[:, k, n * 512:(n + 1) * 512],
                                 start=(k == 0), stop=(k == 15))
                return r
            P.op("pe", mm, [B_wkr] + [B_xTc[k][n] for k in range(16)], [PB[bb]])
        P.op("dve", lambda e: e.tensor_tensor(t1k[:], PS(b1, 0, 512, 0, 64), CCc[:], ALU.mult), [PB[b1], B_tab], [B_t12])
        P.op("dve", lambda e: e.tensor_tensor(t2k[:], PS(b2, 0, 512, 0, 64), SSc[:], ALU.mult), [PB[b2], B_tab], [B_t12])
        P.op("dve", lambda e: e.tensor_tensor(krT[:, n * 512:(n + 1) * 512], t1k[:], t2k[:], ALU.add), [B_t12], [B_big])
    P.barrier()
    AR.release("wkr", "wkrs", "tbf0", "tbf1", "tbf2", "tbi", "CCc", "SSc", "t1k", "t2k")
    stg = [AR.alloc(f"stg{i}", [128, 16, 256], BF16) for i in range(2)]
    B_stg = [Buf() for _ in range(2)]
    ch_stg = [P.chan(f"stg{i}") for i in range(2)]

    jobs = [("k", g, 1024 + 256 * g) for g in range(4)] + [("v", g, 2048 + 256 * g) for g in range(4)]
    jobs += [("ckv", 0, 3584), ("cq", 0, 3072), ("cq", 1, 3328)]

    def load_job(ji):
        if ji < len(jobs):
            c0 = jobs[ji][2]
            P.dma("pool", stg[ji % 2][:], w_in_v[:, :, c0:c0 + 256], [], [B_stg[ji % 2]], ch_stg[ji % 2])

    load_job(0)
    rot = [0]

    def nbank(lo_=0, n_=8):
        b = lo_ + rot[0] % n_
        rot[0] += 1
        return b

    def evac_copy(o_ap, i_ap, rd, wr):
        global_ev[0] += 1
        if global_ev[0] % 2 == 0:
            P.op("act", lambda e: e.activation(o_ap, i_ap, AF.Copy), rd, wr)
        else:
            P.op("dve", lambda e: e.tensor_copy(o_ap, i_ap), rd, wr)

    global_ev = [0]
    sqk1 = AR.alloc("sqk0", [128, 4 * CHP], BF16)
    sqk = [sqk1, sqk1]
    B_sqk1 = Buf()
    B_sqk = [B_sqk1, B_sqk1]
    rbk1 = AR.alloc("rbk0", [128, 512], F32)
    rbk = [rbk1, rbk1]
    B_rbk1 = Buf()
    B_rbk = [B_rbk1, B_rbk1]
    for ji, (kind, g, c0) in enumerate(jobs):
        if kind == "cq" and g == 1:
            continue
        load_job(ji + 1)
        st = stg[ji % 2]
        Bst = B_stg[ji % 2]
        if kind == "k":
            for m in range(2):
                h = 2 * g + m
                for n in range(4):
                    b = nbank()

                    def mm(e):
                        r = None
                        for k in range(16):
                            r = e.matmul(PS(b, 0, 512), st[:, k, m * 128:(m + 1) * 128], xT_ctx[:, k, n * 512:(n + 1) * 512],
                                         start=(k == 0), stop=(k == 15))
                        return r
                    P.op("pe", mm, [Bst] + [B_xTc[k][n] for k in range(16)], [PB[b]])
                    evac_copy(kT_sb[:, h, n * 512:(n + 1) * 512], PS(b, 0, 512), [PB[b]], [B_big])
        elif kind == "v":
            for tb in range(16):
                b = nbank()

                def mm(e):
                    r = None
                    for k in range(16):
                        r = e.matmul(PS(b, 0, 256), xT_ctx[:, k, tb * 128:(tb + 1) * 128], st[:, k, :],
                                     start=(k == 0), stop=(k == 15))
                    return r
                P.op("pe", mm, [Bst] + [B_xTc[k][tb // 4] for k in range(16)], [PB[b]])
                evac_copy(v_sb[:, tb, g * 256:(g + 1) * 256], PS(b, 0, 256), [PB[b]], [B_big])
        elif kind == "ckv":
            for n in range(4):
                bs = [0, 1, 2] if n % 2 == 0 else [3, 4, 5]
                sq, Bsq, rb, Brb = sqk[n % 2], B_sqk[n % 2], rbk[n % 2], B_rbk[n % 2]
                for j in range(2):
                    def mm(e):
                        r = None
                        for k in range(16):
                            r = e.matmul(PS(bs[j], 0, 512), st[:, k, j * 128:(j + 1) * 128], xT_ctx[:, k, n * 512:(n + 1) * 512],
                                         start=(k == 0), stop=(k == 15))
                        return r
                    P.op("pe", mm, [Bst] + [B_xTc[k][n] for k in range(16)], [PB[bs[j]]])
                for j in range(2):
                    P.op("act", lambda e: e.activation(sq[:, j * 512:(j + 1) * 512], PS(bs[j], 0, 512), AF.Square), [PB[bs[j]]], [Bsq])

                def mms(e):
                    e.matmul(PS(bs[2], 0, 512), cbf[:, ONES, :], sq[:, 0:512], start=True, stop=False)
                    return e.matmul(PS(bs[2], 0, 512), cbf[:, ONES, :], sq[:, 512:1024], start=False, stop=True)
                P.op("pe", mms, [Bsq, B_const], [PB[bs[2]]])
                rsqrt_ops(rb[:], PS(bs[2], 0, 512), 1.0 / 256, [PB[bs[2]]], [Brb])
                for j in range(2):
                    P.op("dve", lambda e: e.scalar_tensor_tensor(ckvnT[:, j, n * 512:(n + 1) * 512], PS(bs[j], 0, 512),
                                                                 gp[:, GCKV + j:GCKV + j + 1], rb[:], ALU.mult, ALU.mult),
                         [PB[bs[j]], Brb, B_const], [B_big])
        elif kind == "cq":
            st2 = stg[(ji + 1) % 2]
            Bst2 = B_stg[(ji + 1) % 2]
            for s in range(4):
                sq, Bsq, rb, Brb = sqk[s % 2], B_sqk[s % 2], rbk[s % 2], B_rbk[s % 2]
                bss = 4 + s % 2
                for j in range(4):
                    stj = st if j < 2 else st2

                    def mm(e):
                        r = None
                        for k in range(16):
                            r = e.matmul(PS(j, 0, CH), stj[:, k, (j % 2) * 128:(j % 2 + 1) * 128], xT_own[:, k, s, 0:CH],
                                         start=(k == 0), stop=(k == 15))
                        return r
                    P.op("pe", mm, [Bst, Bst2] + [B_xTo[k][s] for k in range(16)], [PB[j]])
                for j in range(4):
                    P.op("act", lambda e: e.activation(sq[:, j * CHP:j * CHP + CH], PS(j, 0, CH), AF.Square), [PB[j]], [Bsq])

                def mms(e):
                    r = None
                    for j in range(4):
                        r = e.matmul(PS(bss, 0, CH), cbf[:, ONES, :], sq[:, j * CHP:j * CHP + CH], start=(j == 0), stop=(j == 3))
                    return r
                P.op("pe", mms, [Bsq, B_const], [PB[bss]])
                rsqrt_ops(rb[:, 0:CH], PS(bss, 0, CH), 1.0 / 512, [PB[bss]], [Brb])
                for j in range(4):
                    P.op("dve", lambda e: e.scalar_tensor_tensor(cqnT[:, j, s, 0:CH], PS(j, 0, CH),
                                                                 gp[:, GCQ + j:GCQ + j + 1], rb[:, 0:CH], ALU.mult, ALU.mult),
                         [PB[j], Brb, B_const], [B_big])

    dump("kT0", kT_sb[:, 0, 0:256], [128, 256], [B_big])
    dump("cqn", cqnT[:, 0, 0, 0:CH], [128, CH], [B_big])
    dump("krT", krT[:, 0:256], [64, 256], [B_big])
    P.barrier()
    AR.release("xT_ctx", "stg0", "stg1", "sqk0", "rbk0")

    if stop_here("A2"):
        return nc
    oT_sb = AR.alloc("oT_sb", [128, 8, 4, CHP], F32)
    B_oT = Buf()
    wq = [AR.alloc(f"wq{i}", [128, 16, 128], BF16) for i in range(2)]
    B_wq = [Buf() for _ in range(2)]
    ch_wq = [P.chan(f"wq{i}") for i in range(2)]
    qT = [AR.alloc(f"qT{i}", [128, 4, CHP], BF16) for i in range(2)]
    B_qT = [Buf() for _ in range(2)]
    msk = AR.alloc("msk", [128, 19, MW], BF16)
    B_msk = Buf()
    chm = P.chan("msk")
    P.dma("sp", msk[:], msb_d.ap(), [], [B_msk], chm)
    EZ = [AR.alloc(f"EZ{i}", [128, 2, CHP], F32) for i in range(3)]
    SPt = [AR.alloc(f"SP{i}", [128, 2, CHP], BF16) for i in range(2)]
    EC = [AR.alloc(f"EC{i}", [128, CH], F32) for i in range(3)]
    Am = [AR.alloc(f"Am{i}", [128, CH], BF16) for i in range(3)]
    B_EZ = [Buf() for _ in range(3)]
    B_SP = [Buf() for _ in range(2)]
    B_EC = [Buf() for _ in range(3)]
    B_Am = [Buf() for _ in range(3)]
    chi = AR.alloc("chi", [1, CH], BF16)
    B_car = Buf()
    ZB, CBK, OB, QB = [0, 1, 2, 3], [4, 5], [6, 6], [7, 7]
    SC_SB = 128 ** -0.5

    def mask_idx(s, kb):
        if s == 0:
            return kb
        first = NKB[s] - 5
        if kb < first:
            return None
        return 4 + 5 * (s - 1) + (kb - first)

    def load_wq(h):
        if h < 8:
            P.dma("pool", wq[h % 2][:], w_in_v[:, :, h * 128:(h + 1) * 128], [], [B_wq[h % 2]], ch_wq[h % 2])

    qrot = [0]

    def qproj(h):
        if h >= 8:
            return
        for s in range(4):
            b = QB[qrot[0] % 2]
            qrot[0] += 1

            def mm(e):
                r = None
                for k in range(16):
                    r = e.matmul(PS(b, 0, CH), wq[h % 2][:, k, :], xT_own[:, k, s, 0:CH], start=(k == 0), stop=(k == 15))
                return r
            P.op("pe", mm, [B_wq[h % 2]] + [B_xTo[k][s] for k in range(16)], [PB[b]])
            P.op("dve", lambda e: e.tensor_scalar(qT[h % 2][:, s, 0:CH], PS(b, 0, CH), SC_SB, None, ALU.mult), [PB[b]], [B_qT[h % 2]])

    import os
    SKIP = os.environ.get("SB_SKIP", "")
    steps = []
    chain = 0
    for h in range(8):
        for s in range(4):
            for kb in range(NKB[s] - 1, -1, -1):
                steps.append((h, s, kb, chain))
            chain += 1

    def zbank(i):
        return 2 * ((i // 2) % 2) + (i % 2)

    def stA(i):
        h, s, kb, c = steps[i]
        zb = zbank(i)
        mi = mask_idx(s, kb)

        def mm(e):
            r = e.matmul(PS(zb, 0, CH), kT_sb[:, h, kb * 128:(kb + 1) * 128], qT[h % 2][:, s, 0:CH], start=True, stop=(mi is None))
            if mi is not None:
                r = e.matmul(PS(zb, 0, CH), cbf[:, IDB, :], msk[:, mi, 0:CH], start=False, stop=True)
            return r
        P.op("pe", mm, [B_big, B_qT[h % 2], B_msk, B_const], [PB[zb]])

    def stBp(u):
        p = u % 2
        w3, w2 = u % 3, u % 2
        zsrc = psA[:, 2 * p:2 * p + 2, 0:CH]
        P.op("act", lambda e: e.activation(EZ[w3][:, :, 0:CH], zsrc, AF.Exp), [PB[2 * p], PB[2 * p + 1]], [B_EZ[w3]])
        P.op("act", lambda e: e.activation(SPt[w2][:, :, 0:CH], EZ[w3][:, :, 0:CH], AF.Ln, bias=1.0), [B_EZ[w3]], [B_SP[w2]])

    def stC(i):
        h, s, kb, c = steps[i]
        z = i % 2
        u = i // 2
        first = (kb == NKB[s] - 1)
        last = (kb == 0)

        P.op("pe", lambda e: e.matmul(PS(CBK[z], 0, CH), cbf[:, NTRI, :], SPt[u % 2][:, i % 2, 0:CH], start=True, stop=first),
             [B_SP[u % 2], B_const], [PB[CBK[z]]])
        if not first:
            P.op("pe", lambda e: e.matmul(PS(CBK[z], 0, CH), cbf[0:1, OROW, :], chi[:], start=False, stop=True),
                 [B_car, B_const], [PB[CBK[z]]])
        if not last:
            P.op("dve", lambda e: e.tensor_copy(chi[:], PS(CBK[z], 0, CH, 0, 1)), [PB[CBK[z]]], [B_car])

    def stD(i):
        z = i % 2
        P.op("act", lambda e: e.activation(EC[z][:], PS(CBK[z], 0, CH), AF.Exp), [PB[CBK[z]]], [B_EC[z]])

    def stE(i):
        z = i % 2
        u = i // 2
        P.op("pool" if z else "dve", lambda e: e.tensor_tensor(Am[z][:], EZ[u % 3][:, i % 2, 0:CH], EC[z][:], ALU.mult),
             [B_EZ[u % 3], B_EC[z]], [B_Am[z]])

    def stF(i):
        h, s, kb, c = steps[i]
        z = i % 2
        first = (kb == NKB[s] - 1)
        last = (kb == 0)
        ob = OB[0]
        P.op("pe", lambda e: e.matmul(PS(ob, 0, CH), v_sb[:, kb, h * 128:(h + 1) * 128], Am[z][:], start=first, stop=last),
             [B_big, B_Am[z]], [PB[ob]])
        if last:
            P.op("dve", lambda e: e.tensor_copy(oT_sb[:, h, s, 0:CH], PS(ob, 0, CH)), [PB[ob]], [B_oT])

    load_wq(0)
    load_wq(1)
    qproj(0)
    nst = len(steps)
    import os
    if os.environ.get("SB_LIMIT"):
        nst = int(os.environ["SB_LIMIT"])
    assert nst % 2 == 0
    for t in range(nst + 7):
        if t < nst:
            h, s, kb, c = steps[t]
            if s == 0 and kb == NKB[0] - 1:
                qproj(h + 1)
            if s == 1 and kb == NKB[1] - 1:
                load_wq(h + 2)
            stA(t)
        pe_warm(WB[0], v_sb[:, 0, 0:512])
        if t >= 2 and t % 2 == 0 and (t - 2) // 2 < nst // 2:
            stBp((t - 2) // 2)
        for lag, st in ((3, stC), (4, stD), (5, stE), (6, stF)):
            if 0 <= t - lag < nst:
                st(t - lag)
    dump("oTsb", oT_sb[:, 0, 0, 0:CH], [128, CH], [B_oT])
    P.barrier()
    AR.release("kT_sb", "v_sb", "xT_own", "wq0", "wq1", "qT0", "qT1", "msk", "EZ0", "EZ1", "SP0", "SP1",
               "EC0", "EC1", "Am0", "Am1", "EZ2", "EC2", "Am2", "chi")

    if stop_here("B"):
        return nc
    tmpf = [AR.alloc(f"tof{i}", [64, 4 * CHP], F32) for i in range(3)]
    tmpi = AR.alloc("toi", [64, 4 * CHP], I32)
    CCo = AR.alloc("CCo", [64, 4 * CHP], F32)
    SSo = AR.alloc("SSo", [64, 4 * CHP], F32)
    B_tabo = Buf()
    make_tables(bass.AP(pos_own, 0, [[0, 64], [1, 4 * CHP]]), 4 * CHP, CCo[:], SSo[:], tmpf, tmpi, B_tabo, chp)
    P.barrier()
    AR.release("tof0", "tof1", "tof2", "toi")
    oT_ml = AR.alloc("oT_ml", [128, 8, 4, CHP], F32)
    msk = AR.alloc("msk2", [128, 19, MW], BF16)
    P.dma("sp", msk[:], mml_d.ap(), [], [B_msk], chm)
    wuq = [AR.alloc(f"wuq{i}", [128, 4, 192], BF16) for i in range(2)]
    wuqs = [AR.alloc(f"wuqs{i}", [128, 4, 64], BF16) for i in range(2)]
    wukv = [AR.alloc(f"wukv{i}", [128, 2, 256], BF16) for i in range(2)]
    B_wm = [Buf() for _ in range(2)]
    ch_wm = [P.chan(f"wm{i}") for i in range(2)]
    qn = [AR.alloc(f"qn{i}", [128, 4, CHP], BF16) for i in range(2)]
    qr = [AR.alloc(f"qr{i}", [64, 4, CHP], BF16) for i in range(2)]
    kn = [AR.alloc(f"kn{i}", [128, S], BF16) for i in range(2)]
    vm = [AR.alloc(f"vm{i}", [128, 16, 128], BF16) for i in range(2)]
    B_hd = [Buf() for _ in range(2)]
    t1q = AR.alloc("t1q", [64, CH], F32)
    t2q = AR.alloc("t2q", [64, CH], F32)
    B_tq = Buf()
    Pm = [AR.alloc(f"Pm{i}", [128, CH], BF16) for i in range(4)]
    B_Pm = [Buf() for _ in range(4)]
    rd = AR.alloc("rd", [128, CH], F32)
    B_rd = Buf()
    w_uq_v = w_uq.ap().rearrange("(j p) c -> p j c", p=128)
    w_ukv_v = w_ukv.ap().rearrange("(j p) c -> p j c", p=128)
    SB_, OBm, DB, PBk = [0, 1], [2, 3], [4, 5], [6, 7]
    SC_ML = 192 ** -0.5

    def load_wm(h):
        if h >= 8:
            return
        i = h % 2
        P.dma("pool", wuq[i][:], w_uq_v[:, :, 192 * h:192 * h + 192], [], [B_wm[i]], ch_wm[i])
        P.dma("pool", wuqs[i][:, :, 0:32], w_uq_v[:, :, 192 * h + 160:192 * h + 192], [], [B_wm[i]], ch_wm[i])
        P.dma("pool", wuqs[i][:, :, 32:64], w_uq_v[:, :, 192 * h + 128:192 * h + 160], [], [B_wm[i]], ch_wm[i])
        P.dma("pool", wukv[i][:], w_ukv_v[:, :, 256 * h:256 * h + 256], [], [B_wm[i]], ch_wm[i])

    prot = [0]

    def pbank():
        b = PBk[prot[0] % 2]
        prot[0] += 1
        return b

    def mproj_units(h):
        units = []
        if h >= 8:
            return units
        i = h % 2

        def u_qn(s):
            b = pbank()

            def mm(e):
                r = None
                for j in range(4):
                    r = e.matmul(PS(b, 0, CH), wuq[i][:, j, 0:128], cqnT[:, j, s, 0:CH], start=(j == 0), stop=(j == 3))
                return r
            P.op("pe", mm, [B_wm[i], B_big], [PB[b]])
            evac_copy(qn[i][:, s, 0:CH], PS(b, 0, CH), [PB[b]], [B_hd[i]])

        def u_rope(s, which):
            wsel, c0, c1, tq, tab = ((wuq[i], 128, 192, t1q, CCo), (wuqs[i], 0, 64, t2q, SSo))[which]
            bb = pbank()

            def mm2(e):
                r = None
                for j in range(4):
                    r = e.matmul(PS(bb, 0, CH, 0, 64), wsel[:, j, c0:c1], cqnT[:, j, s, 0:CH], start=(j == 0), stop=(j == 3))
                return r
            P.op("pe", mm2, [B_wm[i], B_big], [PB[bb]])
            P.op("dve", lambda e: e.tensor_tensor(tq[:], PS(bb, 0, CH, 0, 64), tab[:, s * CHP:s * CHP + CH], ALU.mult),
                 [PB[bb], B_tabo], [B_tq])
            if which == 1:
                P.op("dve", lambda e: e.tensor_tensor(qr[i][:, s, 0:CH], t1q[:], t2q[:], ALU.add), [B_tq], [B_hd[i]])

        def u_kn(n):
            b = pbank()

            def mm(e):
                e.matmul(PS(b, 0, 512), wukv[i][:, 0, 0:128], ckvnT[:, 0, n * 512:(n + 1) * 512], start=True, stop=False)
                return e.matmul(PS(b, 0, 512), wukv[i][:, 1, 0:128], ckvnT[:, 1, n * 512:(n + 1) * 512], start=False, stop=True)
            P.op("pe", mm, [B_wm[i], B_big], [PB[b]])
            evac_copy(kn[i][:, n * 512:(n + 1) * 512], PS(b, 0, 512), [PB[b]], [B_hd[i]])

        def u_vm(k4):
            b = pbank()

            def mm(e):
                r = None
                for q in range(4):
                    kb = 4 * k4 + q
                    e.matmul(PS(b, q * 128, (q + 1) * 128), ckvnT[:, 0, kb * 128:(kb + 1) * 128], wukv[i][:, 0, 128:256], start=True, stop=False)
                    r = e.matmul(PS(b, q * 128, (q + 1) * 128), ckvnT[:, 1, kb * 128:(kb + 1) * 128], wukv[i][:, 1, 128:256], start=False, stop=True)
                return r
            P.op("pe", mm, [B_wm[i], B_big], [PB[b]])
            evac_copy(vm[i][:, 4 * k4:4 * k4 + 4, :], PSV(b, [[128, 4], [1, 128]]), [PB[b]], [B_hd[i]])

        for s in range(4):
            units.append(lambda s=s: u_qn(s))
            units.append(lambda s=s: u_rope(s, 0))
            units.append(lambda s=s: u_rope(s, 1))
        for n in range(4):
            units.append(lambda n=n: u_kn(n))
        for k4 in range(4):
            units.append(lambda k4=k4: u_vm(k4))
        return units

    msteps = []
    chain = 0
    for h in range(8):
        for s in range(4):
            for kb in range(NKB[s]):
                msteps.append((h, s, kb, chain))
            chain += 1

    def mlA(i):
        h, s, kb, c = msteps[i]
        z = i % 2
        hi = h % 2
        mi = mask_idx(s, kb)

        def mm(e):
            e.matmul(PS(SB_[z], 0, CH), kn[hi][:, kb * 128:(kb + 1) * 128], qn[hi][:, s, 0:CH], start=True, stop=False)
            r = e.matmul(PS(SB_[z], 0, CH), krT[:, kb * 128:(kb + 1) * 128], qr[hi][:, s, 0:CH], start=False, stop=(mi is None))
            if mi is not None:
                r = e.matmul(PS(SB_[z], 0, CH), cbf[:, IDB, :], msk[:, mi, 0:CH], start=False, stop=True)
            return r
        P.op("pe", mm, [B_hd[hi], B_big, B_msk, B_const], [PB[SB_[z]]])

    def mlB(i):
        z, w = i % 2, i % 4
        P.op("act", lambda e: e.activation(Pm[w][:], PS(SB_[z], 0, CH), AF.Exp, scale=SC_ML), [PB[SB_[z]]], [B_Pm[w]])

    def mlC(i):
        h, s, kb, c = msteps[i]
        w = i % 4
        hi = h % 2
        first = (kb == 0)
        last = (kb == NKB[s] - 1)
        ob, db = OBm[c % 2], DB[c % 2]

        def mm(e):
            e.matmul(PS(ob, 0, CH), vm[hi][:, kb, :], Pm[w][:], start=first, stop=last)
            return e.matmul(PS(db, 0, CH), cbf[:, ONES, :], Pm[w][:], start=first, stop=last)
        P.op("pe", mm, [B_hd[hi], B_Pm[w], B_const], [PB[ob], PB[db]])
        if last:
            P.op("act", lambda e: e.activation(rd[:], PS(db, 0, CH), AF.Ln), [PB[db]], [B_rd])
            P.op("act", lambda e: e.activation(rd[:], rd[:], AF.Exp, scale=-1.0), [B_rd], [B_rd])
            P.op("dve", lambda e: e.tensor_tensor(oT_ml[:, h, s, 0:CH], PS(ob, 0, CH), rd[:], ALU.mult), [PB[ob], B_rd], [B_oT])

    load_wm(0)
    load_wm(1)
    for u in mproj_units(0):
        u()
    nst = len(msteps)
    pend_units = []
    hold = 0
    for t in range(nst + 3):
        if t < nst:
            h, s, kb, c = msteps[t]
            if s == 0 and kb == 0:
                pend_units = mproj_units(h + 1)
                hold = 4
            if s == 2 and kb == 0:
                load_wm(h + 2)
            mlA(t)
        if 0 <= t - 1 < nst:
            mlB(t - 1)
        if 0 <= t - 3 < nst:
            mlC(t - 3)
        if hold > 0:
            hold -= 1
        elif pend_units:
            pend_units.pop(0)()
    dump("oTml", oT_ml[:, 0, 0, 0:CH], [128, CH], [B_oT])
    P.barrier()
    AR.release("cqnT", "ckvnT", "krT", "msk2", "CCo", "SSo", "wuq0", "wuq1",
               "wuqs0", "wuqs1", "wukv0", "wukv1", "qn0", "qn1", "qr0", "qr1", "kn0", "kn1", "vm0", "vm1",
               "t1q", "t2q", "Pm0", "Pm1", "Pm2", "Pm3", "rd")

    if stop_here("C"):
        return nc
    wo = AR.alloc("wo", [128, 16, D], BF16)
    B_woq = [Buf() for _ in range(4)]
    w_o_v = w_o.ap().rearrange("(k p) c -> p k c", p=128)
    for q in range(4):
        P.dma("pool", wo[:, 4 * q:4 * q + 4, :], w_o_v[:, 4 * q:4 * q + 4, :], [], [B_woq[q]], P.chan(f"wo{q}"))
    B_wo = B_woq[0]
    mixT = AR.alloc("mixT", [128, 16, 1024], BF16)
    mixH = AR.alloc("mixH", [128, 16, 128], BF16)
    B_mix = Buf()
    P.op("dve", lambda e: e.memset(mixH[:], 0.0), [], [B_mix])
    sqd = [AR.alloc(f"sqd{i}", [128, CH], BF16) for i in range(2)]
    B_sqd = [Buf() for _ in range(2)]
    rbd = [AR.alloc(f"rbd{i}", [128, CH], F32) for i in range(2)]
    B_rbd = [Buf() for _ in range(2)]
    it = 0
    for s in range(4):
        for grp, oT, gcol in ((0, oT_sb, GOSB), (1, oT_ml, GOMLA)):
            bss = it % 2
            rb, Brb = rbd[it % 2], B_rbd[it % 2]
            it += 1
            for h in range(8):
                z = h % 2
                P.op("act", lambda e: e.activation(sqd[z][:], oT[:, h, s, 0:CH], AF.Square), [B_oT], [B_sqd[z]])
                P.op("pe", lambda e: e.matmul(PS(bss, 0, CH), cbf[:, ONES, :], sqd[z][:], start=(h == 0), stop=(h == 7)),
                     [B_sqd[z], B_const], [PB[bss]])
            rsqrt_ops(rb[:], PS(bss, 0, CH), 1.0 / 1024, [PB[bss]], [Brb])
            for h in range(8):
                f = grp * 8 + h
                P.op("dve", lambda e: e.scalar_tensor_tensor(mixT[:, f, s * 256:(s + 1) * 256], oT[:, h, s, 2:CH],
                                                             gp[:, gcol + h:gcol + h + 1], rb[:, 2:CH], ALU.mult, ALU.mult),
                     [B_oT, Brb, B_const], [B_mix])
                P.op("dve", lambda e: e.scalar_tensor_tensor(mixH[:, f, 32 * s:32 * s + 2], oT[:, h, s, 0:2],
                                                             gp[:, gcol + h:gcol + h + 1], rb[:, 0:2], ALU.mult, ALU.mult),
                     [B_oT, Brb, B_const], [B_mix])
    P.barrier()
    AR.release("oT_sb", "oT_ml", "sqd0", "sqd1", "rbd0", "rbd1")

    if stop_here("D1"):
        return nc
    h2T = AR.alloc("h2T", [128, 16, 4, CHP], BF16)
    B_h2 = Buf()
    gpost_b = AR.alloc("gpost_b", [128, D], F32)
    gffn_b = AR.alloc("gffn_b", [128, D], F32)
    P.dma("sp", gpost_b[:], bass.AP(grow, D, [[0, 128], [1, D]]), [], [B_gb], chg)
    P.dma("sp", gffn_b[:], bass.AP(grow, 2 * D, [[0, 128], [1, D]]), [], [B_gb], chg)
    xr = [AR.alloc(f"xr{i}", [128, D], F32) for i in range(2)]
    B_xr = [Buf() for _ in range(2)]
    ch_xr = [P.chan(f"xr{i}") for i in range(2)]
    xnb = [AR.alloc(f"xnb{i}", [128, D], BF16) for i in range(2)]
    B_xnb = [Buf() for _ in range(2)]
    t1 = [AR.alloc(f"t1_{i}", [128, D], F32) for i in range(2)]
    B_t1 = [Buf() for _ in range(2)]
    ch_x1 = [P.chan(f"x1w{i}") for i in range(2)]
    B_x1s = [Buf() for _ in range(8)]
    wu = [[AR.alloc(f"wu0_{hh}", [128, 16, 128], BF16) for hh in range(2)]]
    B_wu = [Buf() for _ in range(3)]
    ch_wu = [P.chan(f"wu{i}") for i in range(3)]
    w_up_v = w_up.ap().rearrange("(k p) c -> p k c", p=128)

    def load_wu(ct):
        if ct < NCT:
            i = ct % 3
            P.dma("pool", wu[i][0][:], w_up_v[:, :, ct * 128:(ct + 1) * 128], [], [B_wu[i]], ch_wu[i])
            P.dma("pool", wu[i][1][:], w_up_v[:, :, DFF + ct * 128:DFF + (ct + 1) * 128], [], [B_wu[i]], ch_wu[i])

    load_wu(0)
    B_h2l = []

    def d2_stageA(tb):
        z = tb % 2
        src = x_halo.ap() if tb == 8 else x_own.ap()[tb * 128:(tb + 1) * 128, :]
        P.dma("sp", xr[z][:], src, [], [B_xr[z]], ch_xr[z])
        for n in range(4):
            def mm(e):
                r = None
                for f in range(16):
                    lhs = mixH[:, f, :] if tb == 8 else mixT[:, f, tb * 128:(tb + 1) * 128]
                    r = e.matmul(PS(n, 0, 512), lhs, wo[:, f, n * 512:(n + 1) * 512], start=(f == 0), stop=(f == 15))
                return r
            P.op("pe", mm, [B_mix] + B_woq, [PB[n]])

    def d2_stageAev(tb):
        z = tb % 2
        c0 = 32 + tb
        cp = 64 + 4 * tb
        for n in range(4):
            P.op("act", lambda e: e.activation(xnb[z][:, n * 512:(n + 1) * 512], PS(n, 0, 512), AF.Square,
                                               accum_out=ssA[:, cp + n:cp + n + 1]), [PB[n]], [B_xnb[z], B_ss[c0]])
            P.op("dve", lambda e: e.tensor_tensor(t1[z][:, n * 512:(n + 1) * 512], PS(n, 0, 512),
                                                  gpost_b[:, n * 512:(n + 1) * 512], ALU.mult), [PB[n], B_gb], [B_t1[z]])
        P.op("dve", lambda e: e.tensor_reduce(ssA[:, c0:c0 + 1], ssA[:, cp:cp + 4], mybir.AxisListType.X, ALU.add),
             [B_ss[c0]], [B_ss[c0]])

    def d2_stageB(tb):
        z = tb % 2
        c0 = 32 + tb
        c1 = 41 + tb
        rsqrt_ops(rsA[:, c0:c0 + 1], ssA[:, c0:c0 + 1], 1.0 / D, [B_ss[c0]], [B_ss[c0]])
        P.op("dve", lambda e: e.scalar_tensor_tensor(t1[z][:], t1[z][:], rsA[:, c0:c0 + 1], xr[z][:], ALU.mult, ALU.add),
             [B_t1[z], B_ss[c0], B_xr[z]], [B_t1[z]])
        if tb < 8:
            P.dma("sp", x1s_d.ap()[tb * 128:(tb + 1) * 128, :], t1[z][:], [B_t1[z]], [B_x1s[tb]], ch_x1[z])
        if tb == 0:
            dump("x1", t1[z][:, 0:256], [128, 256], [B_t1[z]])
        P.op("act", lambda e: e.activation(xnb[z][:], t1[z][:], AF.Square, accum_out=ssA[:, c1:c1 + 1]), [B_t1[z]], [B_xnb[z], B_ss[c1]])
        rsqrt_ops(rsA[:, c1:c1 + 1], ssA[:, c1:c1 + 1], 1.0 / D, [B_ss[c1]], [B_ss[c1]])
        P.op("dve", lambda e: e.scalar_tensor_tensor(xnb[z][:], t1[z][:], rsA[:, c1:c1 + 1], gffn_b[:], ALU.mult, ALU.mult),
             [B_t1[z], B_ss[c1], B_gb], [B_xnb[z]])
        for q in range(4):
            b = 4 + q

            def tr(e):
                r = None
                for kk in range(4):
                    k = 4 * q + kk
                    r = e.transpose(PSB(b)[:, kk * 128:(kk + 1) * 128], xnb[z][:, k * 128:(k + 1) * 128], cbf[:, IDB, :])
                return r
            P.op("pe", tr, [B_xnb[z], B_const], [PB[b]])
            if tb < 8:
                s_, off = tb // 2, 2 + 128 * (tb % 2)
                o_ap = h2T[:, 4 * q:4 * q + 4, s_, off:off + 128]
                i_ap = PSB(b)[:, 0:512].rearrange("p (a b) -> p a b", b=128)
            else:
                o_ap = h2T[:, 4 * q:4 * q + 4, :, 0:2]
                i_ap = PSB(b)[:, 0:512].rearrange("p (k s j) -> p k s j", k=4, s=4, j=32)[:, :, :, 0:2]
            bh = Buf()
            B_h2l.append(bh)
            evac_copy(o_ap, i_ap, [PB[b]], [bh])

    d2_stageA(0)
    d2_stageAev(0)
    for tb in range(9):
        if tb + 1 < 9:
            d2_stageA(tb + 1)
        d2_stageB(tb)
        if tb + 1 < 9:
            d2_stageAev(tb + 1)
    hfv = bass.AP(hf, 0, [[8, 128], [0, 16], [2, 4], [1, 2]])
    P.op("dve", lambda e: e.tensor_tensor(h2T[:, :, :, 0:2], h2T[:, :, :, 0:2], hfv, ALU.mult), B_h2l + [B_const], B_h2l)
    P.barrier()
    AR.release("wo", "mixT", "mixH", "gpost_b", "gffn_b", "xr0", "xr1", "t1_0", "t1_1", "xnb0", "xnb1")

    if stop_here("D2"):
        return nc
    G = AR.alloc("G", [128, NCT, 1024], BF16, top=True)
    B_G = Buf()
    wu.append([AR.alloc(f"wu1_{hh}", [128, 16, 128], BF16) for hh in range(2)])
    wu.append([AR.alloc(f"wu2_{hh}", [128, 16, 128], BF16) for hh in range(2)])
    load_wu(1)
    A0 = [AR.alloc(f"A0_{i}", [128, 4, CHP], F32) for i in range(2)]
    A1 = [AR.alloc(f"A1_{i}", [128, 4, 256], F32) for i in range(2)]
    Y = [AR.alloc(f"Y_{i}", [128, 4, 256], F32) for i in range(2)]
    Gg = AR.alloc("Gg", [128, 4, 256], F32)
    B_A0 = [Buf() for _ in range(2)]
    B_A1 = [Buf() for _ in range(2)]
    B_Y = [Buf() for _ in range(2)]
    B_Gg = Buf()
    for ct in range(NCT):
        load_wu(ct + 2)
        i = ct % 3
        for hh in range(2):
            base = 4 * hh
            for s in range(4):
                def mm(e):
                    r = None
                    for k in range(16):
                        r = e.matmul(PS(base + s, 0, CH), wu[i][hh][:, k, :], h2T[:, k, s, 0:CH], start=(k == 0), stop=(k == 15))
                    return r
                P.op("pe", mm, [B_wu[i], B_h2], [PB[base + s]])
            U = PST(base)
            col = hh * NCT + ct
            pbs = PB[base:base + 4]
            P.op("act", lambda e: e.activation(A0[hh][:, :, 0:CH], U[:, :, 0:CH], AF.Identity, scale=gp[:, CW2 + col:CW2 + col + 1],
                                               bias=gp[:, CB + col:CB + col + 1]), pbs + [B_const], [B_A0[hh]])
            P.op("dve", lambda e: e.scalar_tensor_tensor(A1[hh][:], U[:, :, 1:257], gp[:, CW1 + col:CW1 + col + 1],
                                                         A0[hh][:, :, 2:CH], ALU.mult, ALU.add), pbs + [B_A0[hh], B_const], [B_A1[hh]])
            P.op("dve", lambda e: e.scalar_tensor_tensor(Y[hh][:], U[:, :, 0:256], gp[:, CW0 + col:CW0 + col + 1],
                                                         A1[hh][:], ALU.mult, ALU.add), pbs + [B_A1[hh], B_const], [B_Y[hh]])
        P.op("act", lambda e: e.activation(Gg[:], Y[0][:], AF.Gelu_apprx_tanh), [B_Y[0]], [B_Gg])
        P.op("dve", lambda e: e.tensor_tensor(G[:, ct, :].rearrange("p (a b) -> p a b", b=256), Gg[:], Y[1][:], ALU.mult),
             [B_Gg, B_Y[1]], [B_G])
    dump("G0", G[:, 0, 0:256], [128, 256], [B_G])
    wd = [AR.alloc("wd0", [128, NCT, 128], BF16, top=True)]
    B_wd = [Buf() for _ in range(2)]
    ch_wd = [P.chan(f"wd{i}") for i in range(2)]
    w_dn_v = w_down.ap().rearrange("(c p) n -> p c n", p=128)

    def load_wd(m):
        if m < 16:
            i = m % 2
            P.dma("pool", wd[i][:, 0:22, :], w_dn_v[:, 0:22, m * 128:(m + 1) * 128], [], [B_wd[i]], ch_wd[i])
            P.dma("pool", wd[i][:, 22:44, :], w_dn_v[:, 22:44, m * 128:(m + 1) * 128], [], [B_wd[i]], ch_wd[i])

    load_wd(0)
    P.barrier()
    AR.release("h2T", "A0_0", "A0_1", "A1_0", "A1_1", "Y_0", "Y_1", "Gg",
               "wu0_0", "wu0_1", "wu1_0", "wu1_1", "wu2_0", "wu2_1")

    if stop_here("E"):
        return nc
    y2T = AR.alloc("y2T", [128, 16, 1024], F32)
    B_y2 = [Buf() for _ in range(16)]
    wd.append(AR.alloc("wd1", [128, NCT, 128], BF16))
    sqF = [AR.alloc(f"sqF{i}", [128, 256], BF16) for i in range(2)]
    B_sqF = [Buf() for _ in range(2)]
    rbF = AR.alloc("rbF", [128, 1024], F32)
    B_rbF = Buf()
    def ssmm(m, s, z):
        P.op("pe", lambda e: e.matmul(PS(4 + s, 0, 256), cbf[:, ONES, :], sqF[z][:], start=(m == 0), stop=(m == 15)),
             [B_sqF[z], B_const], [PB[4 + s]])

    pendF = [None]
    it = 0
    for m in range(16):
        load_wd(m + 1)
        i = m % 2
        for s in range(4):
            b = s
            z = it % 2
            it += 1

            def mm(e):
                r = None
                for c in range(NCT):
                    r = e.matmul(PS(b, 0, 256), wd[i][:, c, :], G[:, c, s * 256:(s + 1) * 256], start=(c == 0), stop=(c == NCT - 1))
                return r
            P.op("pe", mm, [B_wd[i], B_G], [PB[b]])
            P.op("act", lambda e: e.activation(y2T[:, m, s * 256:(s + 1) * 256], PS(b, 0, 256), AF.Copy), [PB[b]], [B_y2[m]])
            P.op("act", lambda e: e.activation(sqF[z][:], PS(b, 0, 256), AF.Square), [PB[b]], [B_sqF[z]])
            if pendF[0] is not None:
                ssmm(*pendF[0])
            pendF[0] = (m, s, z)
    ssmm(*pendF[0])
    for s in range(4):
        rsqrt_ops(rbF[:, s * 256:(s + 1) * 256], PS(4 + s, 0, 256), 1.0 / D, [PB[4 + s]], [B_rbF])
    P.barrier()
    AR.release("G", "wd0", "wd1", "sqF0", "sqF1")

    if stop_here("F"):
        return nc
    x1r = [AR.alloc(f"x1r{i}", [128, D], F32) for i in range(2)]
    B_x1r = [Buf() for _ in range(2)]
    ch_x1r = [P.chan(f"x1r{i}") for i in range(2)]
    ot = [AR.alloc(f"ot{i}", [128, D], F32) for i in range(2)]
    B_ot = [Buf() for _ in range(2)]
    ch_ot = [P.chan(f"ot{i}") for i in range(2)]
    for m in range(16):
        P.op("dve", lambda e: e.scalar_tensor_tensor(y2T[:, m, :], y2T[:, m, :], gp[:, GFP + m:GFP + m + 1], rbF[:], ALU.mult, ALU.mult),
             [B_y2[m], B_rbF, B_const], [B_y2[m]])
    outs = []
    for tb in range(8):
        z = tb % 2
        base = 4 * z
        P.dma("sp", x1r[z][:], x1s_d.ap()[tb * 128:(tb + 1) * 128, :], [B_x1s[tb]], [B_x1r[z]], ch_x1r[z])
        for q in range(4):
            b = base + q

            def tr(e):
                r = None
                for mm_ in range(4):
                    m = 4 * q + mm_
                    r = e.transpose(PS(b, mm_ * 128, (mm_ + 1) * 128), y2T[:, m, tb * 128:(tb + 1) * 128], ident[:])
                return r
            P.op("pe", tr, [B_y2[4 * q + mm_] for mm_ in range(4)] + [B_const], [PB[b]])
        yv = PST(base)[:, :, :]
        P.op("dve", lambda e: e.tensor_tensor(ot[z][:].rearrange("p (a b) -> p a b", b=512), yv,
                                              x1r[z][:].rearrange("p (a b) -> p a b", b=512), ALU.add),
             PB[base:base + 4] + [B_x1r[z]], [B_ot[z]])
        outs.append(P.dma("sp", out_d.ap()[tb * 128:(tb + 1) * 128, :], ot[z][:], [B_ot[z]], [], ch_ot[z]))
    P.finish(outs + list(dbg_outs.values()))
    return nc


_NC_CACHE = {}
_PREP_ONLY = False


def _const_inputs():
    ident = np.eye(128, dtype=np.float32)
    cbf = np.zeros((128, 4, 128), dtype=np.float32)
    cbf[:, 0, :] = np.eye(128)
    cbf[:, 1, :] = 1.0
    j = np.arange(128)[:, None]
    k = np.arange(128)[None, :]
    cbf[:, 2, :] = np.where(j >= k, -1.0, 0.0)
    cbf[0, 3, :] = 1.0
    return ident, cbf.astype(ml_dtypes.bfloat16)


def _masks(r):
    msb = np.zeros((128, 19, MW), dtype=np.float32)
    mml = np.zeros((128, 19, MW), dtype=np.float32)
    col = np.arange(CH)
    for s in range(4):
        ch = CHUNKS[r][s]
        qpos = np.where(col < 2, 256 * ch - 2 + col, 256 * ch + col - 2)
        valid = qpos >= 0
        kbs = [0, 1, 2, 3] if s == 0 else list(range(NKB[s] - 5, NKB[s]))
        for kb in kbs:
            ti = kb if s == 0 else 4 + 5 * (s - 1) + (kb - (NKB[s] - 5))
            kpos = 128 * kb + np.arange(128)
            m1 = (kpos[:, None] >= qpos[None, :]) | (~valid)[None, :]
            m2 = ((kpos[:, None] // 64) > (qpos[None, :] // 64)) & valid[None, :]
            msb[:, ti, 0:CH] = np.where(m1, NEG, 0.0)
            mml[:, ti, 0:CH] = np.where(m2, NEG, 0.0)
    return msb.astype(ml_dtypes.bfloat16), mml.astype(ml_dtypes.bfloat16)


def kernel(x, positions, g_attn_pre, w_in, g_cq, w_uq, g_ckv, w_ukv, g_out_sb, g_out_mla, w_o, g_attn_post,
           g_ffn_pre, w_up, conv_w, conv_b, w_down, g_ffn_post):
    x = np.asarray(x, dtype=np.float32)
    positions = np.asarray(positions, dtype=np.int32)
    f = lambda a: np.ascontiguousarray(np.asarray(a, dtype=np.float32))
    ident, cbf = _const_inputs()
    gp = np.zeros((128, NG), dtype=np.float32)
    gp[:, GCQ:GCQ + 4] = f(g_cq)[0].reshape(4, 128).T
    gp[:, GCKV:GCKV + 2] = f(g_ckv)[0].reshape(2, 128).T
    gp[:, GOSB:GOSB + 8] = f(g_out_sb)[0].reshape(8, 128).T
    gp[:, GOMLA:GOMLA + 8] = f(g_out_mla)[0].reshape(8, 128).T
    gp[:, GFP:GFP + 16] = f(g_ffn_post)[0].reshape(16, 128).T
    gp[:, CB:CB + 88] = f(conv_b)[0].reshape(88, 128).T
    cw = f(conv_w)[0]
    gp[:, CW0:CW0 + 88] = cw[0].reshape(88, 128).T
    gp[:, CW1:CW1 + 88] = cw[1].reshape(88, 128).T
    gp[:, CW2:CW2 + 88] = cw[2].reshape(88, 128).T
    half = 32
    inv_freq = (np.float32(10000.0) ** (-np.arange(half, dtype=np.float32) / np.float32(half))).astype(np.float32)
    gp[0:32, INVF] = inv_freq
    gp[32:64, INVF] = inv_freq
    gp[0:32, SGN] = -1.0
    gp[32:64, SGN] = 1.0
    grow = np.stack([f(g_attn_pre)[0], f(g_attn_post)[0], f(g_ffn_pre)[0]], 0)
    weights = dict(w_in=f(w_in)[0], w_uq=f(w_uq)[0], w_ukv=f(w_ukv)[0], w_o=f(w_o)[0], w_up=f(w_up)[0], w_down=f(w_down)[0])
    masks = [_masks(0), _masks(1)]
    in_maps = []
    own_rows = []
    for core in range(8):
        b, r = core // 2, core % 2
        chunks = CHUNKS[r]
        idx = np.concatenate([np.arange(256 * ch, 256 * ch + 256) for ch in chunks])
        own_rows.append(idx)
        x_halo = np.zeros((128, D), dtype=np.float32)
        pos_own = np.zeros((1, 4 * CHP), dtype=np.int32)
        hflag = np.zeros((128, 8), dtype=np.float32)
        for s, ch in enumerate(chunks):
            pos_own[0, s * CHP + 2:s * CHP + CH] = positions[b, 256 * ch:256 * ch + 256]
            if ch > 0:
                x_halo[32 * s:32 * s + 2] = x[b, 256 * ch - 2:256 * ch]
                pos_own[0, s * CHP:s * CHP + 2] = positions[b, 256 * ch - 2:256 * ch]
                hflag[:, 2 * s:2 * s + 2] = 1.0
        m = dict(x_ctx=np.ascontiguousarray(x[b]), x_own=np.ascontiguousarray(x[b][idx]), x_halo=x_halo,
                 pos_ctx=np.ascontiguousarray(positions[b][None, :]), pos_own=pos_own, gp=gp, grow=grow,
                 ident=ident, cbf=cbf, msb=masks[r][0], mml=masks[r][1], hflag=hflag)
        m.update(weights)
        in_maps.append(m)
    if _PREP_ONLY:
        return in_maps, own_rows
    if "nc" not in _NC_CACHE:
        _NC_CACHE["nc"] = build_program()
    res = run_bass_kernel_spmd(_NC_CACHE["nc"], in_maps, core_ids=list(range(8)))
    out = np.zeros((NB, S, D), dtype=np.float32)
    for core in range(8):
        b = core // 2
        out[b, own_rows[core]] = res.results[core]["out"]
    if DEBUG:
        kernel.last_results = res.results
    return out
```

```python
import math
import numpy as np
import ml_dtypes
import concourse.bass as bass
import concourse.mybir as mybir
from concourse.bass_utils import run_bass_kernel_spmd

F32 = mybir.dt.float32
BF16 = mybir.dt.bfloat16
I32 = mybir.dt.int32
AF = mybir.ActivationFunctionType
ALU = mybir.AluOpType
PI = math.pi

D = 2048
S = 2048
NB = 4
CH = 258
MW = 272
CHP = 272
DFF = 5632
NCT = 44
EPS = 1e-6
NEG = -30000.0
CHUNKS = ([0, 2, 5, 7], [1, 3, 4, 6])
NKB = [4, 8, 12, 16]

GCQ, GCKV, GOSB, GOMLA, GFP, CB, CW0, CW1, CW2, INVF, SGN, NG = 0, 4, 6, 14, 22, 38, 126, 214, 302, 390, 391, 392

DEBUG = {}


class Buf:
    __slots__ = ("w", "r", "excl", "acc")

    def __init__(self, excl=False):
        self.w = None
        self.r = {}
        self.excl = excl
        self.acc = {}


class Chan:
    __slots__ = ("sem", "count", "last")

    def __init__(self, sem):
        self.sem = sem
        self.count = 0
        self.last = None


class Prog:
    def __init__(self, nc):
        self.nc = nc
        self.eng = {"pe": nc.tensor, "act": nc.scalar, "dve": nc.vector, "pool": nc.gpsimd, "sp": nc.sync}
        self.sem = {}
        self.cnt = {}
        self.seen = {}
        for e in self.eng:
            self.sem[e] = nc.alloc_semaphore("prog_" + e)
            self.cnt[e] = 0
            self.seen[e] = {}
        self.chans = []
        self.nops = 0
        self.nwaits = 0

    def chan(self, name):
        c = Chan(self.nc.alloc_semaphore("ch_" + name))
        self.chans.append(c)
        return c

    def _wait_need(self, e, need):
        seen = self.seen[e]
        for sem, val in need.items():
            if seen.get(sem, 0) >= val:
                continue
            self.eng[e].wait_ge(sem, val)
            seen[sem] = val
            self.nwaits += 1

    def _collect(self, e, reads, writes, extra=()):
        need = {}
        pe_sem = self.sem["pe"]

        def add(sem, val):
            if e == "pe" and sem is pe_sem:
                return
            if need.get(sem, 0) < val:
                need[sem] = val

        own = self.sem.get(e)
        for b in reads:
            if b.excl:
                for sem, val in b.acc.items():
                    if sem is not own:
                        add(sem, val)
            elif b.w is not None:
                add(*b.w)
        for b in writes:
            if b.excl:
                for sem, val in b.acc.items():
                    if sem is not own:
                        add(sem, val)
                continue
            if b.w is not None:
                add(*b.w)
            for sem, val in b.r.items():
                add(sem, val)
        for t in extra:
            if t is not None:
                add(*t)
        self._wait_need(e, need)

    def _record(self, tok, reads, writes):
        sem, val = tok
        for b in reads:
            if b.excl:
                b.acc[sem] = val
            elif b.r.get(sem, 0) < val:
                b.r[sem] = val
        for b in writes:
            if b.excl:
                b.acc[sem] = val
                continue
            b.w = tok
            b.r = {}

    def op(self, e, fn, reads=(), writes=()):
        self._collect(e, reads, writes)
        ins = fn(self.eng[e])
        self.cnt[e] += 1
        tok = (self.sem[e], self.cnt[e])
        ins.then_inc(self.sem[e], 1)
        self._record(tok, reads, writes)
        self.nops += 1
        return tok

    def dma(self, q, out, in_, reads, writes, chan):
        self._collect(q, reads, writes, extra=(chan.last,))
        ins = self.eng[q].dma_start(out=out, in_=in_)
        chan.count += 1
        tok = (chan.sem, 16 * chan.count)
        ins.then_inc(chan.sem, 16)
        chan.last = tok
        self._record(tok, reads, writes)
        self.nops += 1
        return tok

    def barrier(self):
        need = {}
        for e in self.eng:
            if self.cnt[e] > 0:
                need[self.sem[e]] = self.cnt[e]
        for c in self.chans:
            if c.last is not None:
                need[c.last[0]] = c.last[1]
        for e in self.eng:
            self._wait_need(e, dict(need))

    def finish(self, toks):
        need = {}
        for t in toks:
            if t is None:
                continue
            if need.get(t[0], 0) < t[1]:
                need[t[0]] = t[1]
        self._wait_need("sp", need)


class Arena:
    def __init__(self, nc, lo, hi):
        self.nc = nc
        self.free = [(lo, hi)]
        self.live = {}
        self.uid = 0

    def alloc(self, name, shape, dtype, top=False):
        bpe = 4 if dtype in (F32, I32) else 2
        n = 1
        for d in shape[1:]:
            n *= d
        size = (n * bpe + 63) // 64 * 64
        order = range(len(self.free) - 1, -1, -1) if top else range(len(self.free))
        for i in order:
            a, b = self.free[i]
            if b - a >= size:
                if top:
                    self.free[i] = (a, b - size)
                    a = b - size
                else:
                    self.free[i] = (a + size, b)
                self.uid += 1
                h = self.nc.alloc_sbuf_tensor_at(f"{name}_{self.uid}", list(shape), dtype, offset=a)
                self.live[name] = (a, size)
                return h
        raise RuntimeError(f"arena OOM for {name} size {size}; free={self.free}")

    def release(self, *names):
        for name in names:
            a, size = self.live.pop(name)
            self.free.append((a, a + size))
        self.free.sort()
        merged = []
        for a, b in self.free:
            if merged and merged[-1][1] == a:
                merged[-1] = (merged[-1][0], b)
            else:
                merged.append((a, b))
        self.free = [(a, b) for a, b in merged if b > a]


def build_program(stop=None):
    nc = bass.Bass("TRN2", target_bir_lowering=False)
    P = Prog(nc)

    def stop_here(tag):
        if stop == tag:
            P.barrier()
            P.finish(list(dbg_outs.values()))
            print("STOP at", tag, "ops", P.nops, "waits", P.nwaits)
            return True
        return False

    def din(name, shape, dtype=F32):
        return nc.dram_tensor(name, list(shape), dtype, kind="ExternalInput")

    x_ctx = din("x_ctx", [S, D])
    x_own = din("x_own", [1024, D])
    x_halo = din("x_halo", [128, D])
    pos_ctx = din("pos_ctx", [1, S], I32)
    pos_own = din("pos_own", [1, 4 * CHP], I32)
    w_in = din("w_in", [D, 3904])
    w_uq = din("w_uq", [512, 1536])
    w_ukv = din("w_ukv", [256, 2048])
    w_o = din("w_o", [D, D])
    w_up = din("w_up", [D, 2 * DFF])
    w_down = din("w_down", [DFF, D])
    gp_d = din("gp", [128, NG])
    grow = din("grow", [3, D])
    ident_d = din("ident", [128, 128])
    cbf_d = din("cbf", [128, 4, 128], BF16)
    msb_d = din("msb", [128, 19, MW], BF16)
    mml_d = din("mml", [128, 19, MW], BF16)
    hf_d = din("hflag", [128, 8])
    out_d = nc.dram_tensor("out", [1024, D], F32, kind="ExternalOutput")
    x1s_d = nc.dram_tensor("x1s", [1024, D], F32)
    dbg_outs = {}

    A = nc.alloc_sbuf_tensor
    ident = A("ident_s", [128, 128], F32)
    cbf = A("cbf_s", [128, 4, 128], BF16)
    gp = A("gp_s", [128, NG], F32)
    eps_t = A("eps_t", [128, 1], F32)
    ssA = A("ssA", [128, 128], F32)
    rsA = A("rsA", [128, 64], F32)
    hf = A("hf_s", [128, 8], F32)
    B_const = Buf()
    chc = P.chan("const")
    P.dma("sp", ident[:], ident_d.ap(), [], [B_const], chc)
    P.dma("sp", cbf[:], cbf_d.ap(), [], [B_const], chc)
    P.dma("sp", gp[:], gp_d.ap(), [], [B_const], chc)
    P.dma("sp", hf[:], hf_d.ap(), [], [B_const], chc)
    P.op("pool", lambda e: e.memset(eps_t[:], EPS), [], [B_const])
    P.op("pool", lambda e: e.memset(ssA[:], 0.0), [], [B_const])
    P.op("pool", lambda e: e.memset(rsA[:], 0.0), [], [B_const])
    IDB, ONES, NTRI, OROW = 0, 1, 2, 3
    P.barrier()

    psA = nc.alloc_psum_tensor("psA", [128, 4, 512], F32)
    psB = nc.alloc_psum_tensor("psB", [128, 4, 512], F32)
    PB = [Buf(excl=True) for _ in range(8)]

    def PST(i):
        return psA if i < 4 else psB

    def PS(i, lo, hi, p0=0, p1=128):
        return PST(i)[p0:p1, i % 4, lo:hi]

    def PSV(i, dims, off=0):
        return bass.AP(PST(i), (i % 4) * 512 + off, [[2048, 128]] + dims)

    dbg_buf = Buf()
    dbg_ch = []
    if DEBUG:
        dbg_tmp = A("dbg_tmp", [128, 264], F32)
        dbg_ch.append(P.chan("dbg"))
    import os
    WB = [int(v) for v in os.environ.get("WARM_B", "1,0,0").split(",")]
    WC = [int(v) for v in os.environ.get("WARM_C", "0,0").split(",")]
    WD = int(os.environ.get("WARM_D", "0"))

    def pe_warm(n, src_ap):
        if n <= 0:
            return

        def mm(e):
            r = None
            for _ in range(n):
                r = e.matmul(PS(7, 0, 512), cbf[:, ONES, :], src_ap, start=True, stop=True)
            return r
        P.op("pe", mm, [], [PB[7]])

    lo = (nc.sbuf_base + 63) // 64 * 64
    AR = Arena(nc, lo, nc.sbuf_top)

    def rsqrt_ops(dst, src, scale, rd, wr):
        P.op("act", lambda e: e.activation(dst, src, AF.Ln, scale=scale, bias=eps_t[:]), rd + [B_const], wr)
        P.op("act", lambda e: e.activation(dst, dst, AF.Exp, scale=-0.5), wr, wr)

    def dump(name, ap_sb, shape, rd):
        if name not in DEBUG:
            return
        t = nc.dram_tensor("dbg_" + name, list(shape), F32, kind="ExternalOutput")
        tv = dbg_tmp[0:shape[0], 0:shape[1]]
        P.op("dve", lambda e: e.tensor_copy(tv, ap_sb), rd, [dbg_buf])
        dbg_outs[name] = P.dma("sp", t.ap(), tv, [dbg_buf], [], dbg_ch[0])

    xT_ctx = AR.alloc("xT_ctx", [128, 16, S], BF16)
    xT_own = AR.alloc("xT_own", [128, 16, 4, CHP], BF16)
    B_xTc = [[Buf() for _ in range(4)] for _ in range(16)]
    B_xTo = [[Buf() for _ in range(4)] for _ in range(16)]
    xa = [AR.alloc(f"xa{i}", [128, D], F32) for i in range(4)]
    B_xa = [Buf() for _ in range(4)]
    ch_xa = [P.chan(f"xa{i}") for i in range(4)]
    xbt = [AR.alloc(f"xbt{i}", [128, D], BF16) for i in range(8)]
    B_xb = [Buf() for _ in range(8)]

    def PSB(i):
        return PST(i)[:, i % 4, :].bitcast(BF16)
    junkA = AR.alloc("junkA", [128, D], BF16)
    B_junk = Buf()
    gpre_b = AR.alloc("gpre_b", [128, D], F32)
    B_gb = Buf()
    chg = P.chan("gb")
    P.dma("sp", gpre_b[:], bass.AP(grow, 0, [[0, 128], [1, D]]), [], [B_gb], chg)
    B_ss = [Buf() for _ in range(64)]
    groups = [("c", g) for g in range(4)] + [("o", g) for g in range(2)] + [("h", 0)]
    blkc = [0]
    evc = [0]

    def a1_stage1(gi):
        kind, g = groups[gi]
        nb = 1 if kind == "h" else 4
        for i in range(nb):
            xi = i
            bi_ = 4 * (gi % 2) + i
            blk = blkc[0]
            blkc[0] += 1
            if kind == "c":
                src = x_ctx.ap()[(4 * g + i) * 128:(4 * g + i + 1) * 128, :]
            elif kind == "o":
                src = x_own.ap()[(4 * g + i) * 128:(4 * g + i + 1) * 128, :]
            else:
                src = x_halo.ap()
            P.dma("sp", xa[xi][:], src, [], [B_xa[xi]], ch_xa[xi])
            P.op("act", lambda e: e.activation(junkA[:], xa[xi][:], AF.Square, accum_out=ssA[:, blk:blk + 1]),
                 [B_xa[xi]], [B_junk, B_ss[blk]])
            rsqrt_ops(rsA[:, blk:blk + 1], ssA[:, blk:blk + 1], 1.0 / D, [B_ss[blk]], [B_ss[blk]])
            P.op("dve", lambda e: e.scalar_tensor_tensor(xbt[bi_][:], xa[xi][:], rsA[:, blk:blk + 1], gpre_b[:], ALU.mult, ALU.mult),
                 [B_xa[xi], B_ss[blk], B_gb], [B_xb[bi_]])

    def a1_stage2(gi):
        kind, g = groups[gi]
        nb = 1 if kind == "h" else 4
        xs_ = [xbt[4 * (gi % 2) + i] for i in range(nb)]
        Bx = [B_xb[4 * (gi % 2) + i] for i in range(nb)]
        for k in range(16):
            bi = k % 8

            def tr(e):
                r = None
                for i in range(nb):
                    r = e.transpose(PSB(bi)[:, i * 128:(i + 1) * 128], xs_[i][:, k * 128:(k + 1) * 128], cbf[:, IDB, :])
                return r
            P.op("pe", tr, Bx + [B_const], [PB[bi]])
            eng = "act" if evc[0] % 2 == 0 else "dve"
            evc[0] += 1
            if kind == "c":
                o_ap, i_ap, wr = xT_ctx[:, k, g * 512:(g + 1) * 512], PSB(bi)[:, 0:512], [B_xTc[k][g]]
            elif kind == "o":
                o_ap = xT_own[:, k, 2 * g:2 * g + 2, 2:CH]
                i_ap = PSB(bi)[:, 0:512].rearrange("p (a b) -> p a b", b=256)
                wr = [B_xTo[k][2 * g], B_xTo[k][2 * g + 1]]
            else:
                o_ap = xT_own[:, k, :, 0:2]
                i_ap = PSB(bi)[:, 0:128].rearrange("p (s j) -> p s j", j=32)[:, :, 0:2]
                wr = [B_xTo[k][s] for s in range(4)]
            if eng == "act":
                P.op("act", lambda e: e.activation(o_ap, i_ap, AF.Copy), [PB[bi]], wr)
            else:
                P.op("dve", lambda e: e.tensor_copy(o_ap, i_ap), [PB[bi]], wr)

    a1_stage1(0)
    for gi in range(len(groups)):
        if gi + 1 < len(groups):
            a1_stage1(gi + 1)
        a1_stage2(gi)
    P.barrier()
    AR.release("xa0", "xa1", "xa2", "xa3", "xbt0", "xbt1", "xbt2", "xbt3", "xbt4", "xbt5", "xbt6", "xbt7", "junkA", "gpre_b")

    if stop_here("A1"):
        return nc
    kT_sb = AR.alloc("kT_sb", [128, 8, S], BF16)
    v_sb = AR.alloc("v_sb", [128, 16, 1024], BF16)
    cqnT = AR.alloc("cqnT", [128, 4, 4, CHP], BF16)
    ckvnT = AR.alloc("ckvnT", [128, 2, S], BF16)
    krT = AR.alloc("krT", [64, S], BF16)
    B_big = Buf()
    wkr = AR.alloc("wkr", [128, 16, 64], BF16)
    wkrs = AR.alloc("wkrs", [128, 16, 64], BF16)
    B_wkr = Buf()
    ch_wkr = P.chan("wkr")
    w_in_v = w_in.ap().rearrange("(k p) c -> p k c", p=128)
    P.dma("pool", wkr[:], w_in_v[:, :, 3840:3904], [], [B_wkr], ch_wkr)
    P.dma("pool", wkrs[:, :, 0:32], w_in_v[:, :, 3872:3904], [], [B_wkr], ch_wkr)
    P.dma("pool", wkrs[:, :, 32:64], w_in_v[:, :, 3840:3872], [], [B_wkr], ch_wkr)

    def make_tables(pos_ap_bcast, ncols, CC, SS, tmpf, tmpi, B_t, chp):
        posi, ang, a2, tf = tmpi, tmpf[0], tmpf[1], tmpf[2]
        P.dma("sp", posi[:, 0:ncols], pos_ap_bcast, [], [B_t], chp)
        P.op("dve", lambda e: e.tensor_copy(ang[:, 0:ncols], posi[:, 0:ncols]), [B_t], [B_t])
        P.op("dve", lambda e: e.tensor_scalar(ang[:, 0:ncols], ang[:, 0:ncols], gp[0:64, INVF:INVF + 1], None, ALU.mult),
             [B_t, B_const], [B_t])
        for shift, dst, sg in ((PI / 2, CC, False), (0.0, SS, True)):
            a2v, tfv, tiv = a2[:, 0:ncols], tf[:, 0:ncols], posi[:, 0:ncols]
            P.op("dve", lambda e: e.tensor_scalar(a2v, ang[:, 0:ncols], shift, None, ALU.add), [B_t], [B_t])
            P.op("dve", lambda e: e.tensor_scalar(tfv, a2v, 1.0 / (2 * PI), None, ALU.mult), [B_t], [B_t])
            P.op("dve", lambda e: e.tensor_copy(tiv, tfv), [B_t], [B_t])
            P.op("dve", lambda e: e.tensor_copy(tfv, tiv), [B_t], [B_t])
            P.op("dve", lambda e: e.scalar_tensor_tensor(a2v, tfv, -2 * PI, a2v, ALU.mult, ALU.add), [B_t], [B_t])
            P.op("dve", lambda e: e.tensor_single_scalar(tfv, a2v, PI, ALU.is_gt), [B_t], [B_t])
            P.op("dve", lambda e: e.scalar_tensor_tensor(a2v, tfv, -2 * PI, a2v, ALU.mult, ALU.add), [B_t], [B_t])
            P.op("dve", lambda e: e.tensor_single_scalar(tfv, a2v, -PI, ALU.is_lt), [B_t], [B_t])
            P.op("dve", lambda e: e.scalar_tensor_tensor(a2v, tfv, 2 * PI, a2v, ALU.mult, ALU.add), [B_t], [B_t])
            P.op("act", lambda e: e.activation(dst, a2v, AF.Sin), [B_t], [B_t])
            if sg:
                P.op("dve", lambda e: e.tensor_scalar(dst, dst, gp[0:64, SGN:SGN + 1], None, ALU.mult), [B_t, B_const], [B_t])

    tmpf = [AR.alloc(f"tbf{i}", [64, 512], F32) for i in range(3)]
    tmpi = AR.alloc("tbi", [64, 512], I32)
    CCc = AR.alloc("CCc", [64, 512], F32)
    SSc = AR.alloc("SSc", [64, 512], F32)
    t1k = AR.alloc("t1k", [64, 512], F32)
    t2k = AR.alloc("t2k", [64, 512], F32)
    B_tab = Buf()
    B_t12 = Buf()
    chp = P.chan("pos")
    for n in range(4):
        make_tables(bass.AP(pos_ctx, n * 512, [[0, 64], [1, 512]]), 512, CCc[:], SSc[:], tmpf, tmpi, B_tab, chp)
        b1, b2 = 6, 7
        for bb, wsel in ((b1, wkr), (b2, wkrs)):
            def mm(e):
                r = None
                for k in range(16):
                    r = e.matmul(PS(bb, 0, 512, 0, 64), wsel[:, k, :], xT_ctx[:, k, n * 512:(n + 1) * 512],
                                 start=(k == 0), stop=(k == 15))
                return r
            P.op("pe", mm, [B_wkr] + [B_xTc[k][n] for k in range(16)], [PB[bb]])
        P.op("dve", lambda e: e.tensor_tensor(t1k[:], PS(b1, 0, 512, 0, 64), CCc[:], ALU.mult), [PB[b1], B_tab], [B_t12])
        P.op("dve", lambda e: e.tensor_tensor(t2k[:], PS(b2, 0, 512, 0, 64), SSc[:], ALU.mult), [PB[b2], B_tab], [B_t12])
        P.op("dve", lambda e: e.tensor_tensor(krT[:, n * 512:(n + 1) * 512], t1k[:], t2k[:], ALU.add), [B_t12], [B_big])
    P.barrier()
    AR.release("wkr", "wkrs", "tbf0", "tbf1", "tbf2", "tbi", "CCc", "SSc", "t1k", "t2k")
    stg = [AR.alloc(f"stg{i}", [128, 16, 256], BF16) for i in range(2)]
    B_stg = [Buf() for _ in range(2)]
    ch_stg = [P.chan(f"stg{i}") for i in range(2)]

    jobs = [("k", g, 1024 + 256 * g) for g in range(4)] + [("v", g, 2048 + 256 * g) for g in range(4)]
    jobs += [("ckv", 0, 3584), ("cq", 0, 3072), ("cq", 1, 3328)]

    def load_job(ji):
        if ji < len(jobs):
            c0 = jobs[ji][2]
            P.dma("pool", stg[ji % 2][:], w_in_v[:, :, c0:c0 + 256], [], [B_stg[ji % 2]], ch_stg[ji % 2])

    load_job(0)
    rot = [0]

    def nbank(lo_=0, n_=8):
        b = lo_ + rot[0] % n_
        rot[0] += 1
        return b

    def evac_copy(o_ap, i_ap, rd, wr):
        global_ev[0] += 1
        if global_ev[0] % 2 == 0:
            P.op("act", lambda e: e.activation(o_ap, i_ap, AF.Copy), rd, wr)
        else:
            P.op("dve", lambda e: e.tensor_copy(o_ap, i_ap), rd, wr)

    global_ev = [0]
    sqk1 = AR.alloc("sqk0", [128, 4 * CHP], BF16)
    sqk = [sqk1, sqk1]
    B_sqk1 = Buf()
    B_sqk = [B_sqk1, B_sqk1]
    rbk1 = AR.alloc("rbk0", [128, 512], F32)
    rbk = [rbk1, rbk1]
    B_rbk1 = Buf()
    B_rbk = [B_rbk1, B_rbk1]
    for ji, (kind, g, c0) in enumerate(jobs):
        if kind == "cq" and g == 1:
            continue
        load_job(ji + 1)
        st = stg[ji % 2]
        Bst = B_stg[ji % 2]
        if kind == "k":
            for m in range(2):
                h = 2 * g + m
                for n in range(4):
                    b = nbank()

                    def mm(e):
                        r = None
                        for k in range(16):
                            r = e.matmul(PS(b, 0, 512), st[:, k, m * 128:(m + 1) * 128], xT_ctx[:, k, n * 512:(n + 1) * 512],
                                         start=(k == 0), stop=(k == 15))
                        return r
                    P.op("pe", mm, [Bst] + [B_xTc[k][n] for k in range(16)], [PB[b]])
                    evac_copy(kT_sb[:, h, n * 512:(n + 1) * 512], PS(b, 0, 512), [PB[b]], [B_big])
        elif kind == "v":
            for tb in range(16):
                b = nbank()

                def mm(e):
                    r = None
                    for k in range(16):
                        r = e.matmul(PS(b, 0, 256), xT_ctx[:, k, tb * 128:(tb + 1) * 128], st[:, k, :],
                                     start=(k == 0), stop=(k == 15))
                    return r
                P.op("pe", mm, [Bst] + [B_xTc[k][tb // 4] for k in range(16)], [PB[b]])
                evac_copy(v_sb[:, tb, g * 256:(g + 1) * 256], PS(b, 0, 256), [PB[b]], [B_big])
        elif kind == "ckv":
            for n in range(4):
                bs = [0, 1, 2] if n % 2 == 0 else [3, 4, 5]
                sq, Bsq, rb, Brb = sqk[n % 2], B_sqk[n % 2], rbk[n % 2], B_rbk[n % 2]
                for j in range(2):
                    def mm(e):
                        r = None
                        for k in range(16):
                            r = e.matmul(PS(bs[j], 0, 512), st[:, k, j * 128:(j + 1) * 128], xT_ctx[:, k, n * 512:(n + 1) * 512],
                                         start=(k == 0), stop=(k == 15))
                        return r
                    P.op("pe", mm, [Bst] + [B_xTc[k][n] for k in range(16)], [PB[bs[j]]])
                for j in range(2):
                    P.op("act", lambda e: e.activation(sq[:, j * 512:(j + 1) * 512], PS(bs[j], 0, 512), AF.Square), [PB[bs[j]]], [Bsq])

                def mms(e):
                    e.matmul(PS(bs[2], 0, 512), cbf[:, ONES, :], sq[:, 0:512], start=True, stop=False)
                    return e.matmul(PS(bs[2], 0, 512), cbf[:, ONES, :], sq[:, 512:1024], start=False, stop=True)
                P.op("pe", mms, [Bsq, B_const], [PB[bs[2]]])
                rsqrt_ops(rb[:], PS(bs[2], 0, 512), 1.0 / 256, [PB[bs[2]]], [Brb])
                for j in range(2):
                    P.op("dve", lambda e: e.scalar_tensor_tensor(ckvnT[:, j, n * 512:(n + 1) * 512], PS(bs[j], 0, 512),
                                                                 gp[:, GCKV + j:GCKV + j + 1], rb[:], ALU.mult, ALU.mult),
                         [PB[bs[j]], Brb, B_const], [B_big])
        elif kind == "cq":
            st2 = stg[(ji + 1) % 2]
            Bst2 = B_stg[(ji + 1) % 2]
            for s in range(4):
                sq, Bsq, rb, Brb = sqk[s % 2], B_sqk[s % 2], rbk[s % 2], B_rbk[s % 2]
                bss = 4 + s % 2
                for j in range(4):
                    stj = st if j < 2 else st2

                    def mm(e):
                        r = None
                        for k in range(16):
                            r = e.matmul(PS(j, 0, CH), stj[:, k, (j % 2) * 128:(j % 2 + 1) * 128], xT_own[:, k, s, 0:CH],
                                         start=(k == 0), stop=(k == 15))
                        return r
                    P.op("pe", mm, [Bst, Bst2] + [B_xTo[k][s] for k in range(16)], [PB[j]])
                for j in range(4):
                    P.op("act", lambda e: e.activation(sq[:, j * CHP:j * CHP + CH], PS(j, 0, CH), AF.Square), [PB[j]], [Bsq])

                def mms(e):
                    r = None
                    for j in range(4):
                        r = e.matmul(PS(bss, 0, CH), cbf[:, ONES, :], sq[:, j * CHP:j * CHP + CH], start=(j == 0), stop=(j == 3))
                    return r
                P.op("pe", mms, [Bsq, B_const], [PB[bss]])
                rsqrt_ops(rb[:, 0:CH], PS(bss, 0, CH), 1.0 / 512, [PB[bss]], [Brb])
                for j in range(4):
                    P.op("dve", lambda e: e.scalar_tensor_tensor(cqnT[:, j, s, 0:CH], PS(j, 0, CH),
                                                                 gp[:, GCQ + j:GCQ + j + 1], rb[:, 0:CH], ALU.mult, ALU.mult),
                         [PB[j], Brb, B_const], [B_big])

    dump("kT0", kT_sb[:, 0, 0:256], [128, 256], [B_big])
    dump("cqn", cqnT[:, 0, 0, 0:CH], [128, CH], [B_big])
    dump("krT", krT[:, 0:256], [64, 256], [B_big])
    P.barrier()
    AR.release("xT_ctx", "stg0", "stg1", "sqk0", "rbk0")

    if stop_here("A2"):
        return nc
    oT_sb = AR.alloc("oT_sb", [128, 8, 4, CHP], F32)
    B_oT = Buf()
    wq = [AR.alloc(f"wq{i}", [128, 16, 128], BF16) for i in range(2)]
    B_wq = [Buf() for _ in range(2)]
    ch_wq = [P.chan(f"wq{i}") for i in range(2)]
    qT = [AR.alloc(f"qT{i}", [128, 4, CHP], BF16) for i in range(2)]
    B_qT = [Buf() for _ in range(2)]
    msk = AR.alloc("msk", [128, 19, MW], BF16)
    B_msk = Buf()
    chm = P.chan("msk")
    P.dma("sp", msk[:], msb_d.ap(), [], [B_msk], chm)
    EZ = [AR.alloc(f"EZ{i}", [128, 2, CHP], F32) for i in range(3)]
    SPt = [AR.alloc(f"SP{i}", [128, 2, CHP], BF16) for i in range(2)]
    EC = [AR.alloc(f"EC{i}", [128, CH], F32) for i in range(3)]
    Am = [AR.alloc(f"Am{i}", [128, CH], BF16) for i in range(3)]
    B_EZ = [Buf() for _ in range(3)]
    B_SP = [Buf() for _ in range(2)]
    B_EC = [Buf() for _ in range(3)]
    B_Am = [Buf() for _ in range(3)]
    chi = AR.alloc("chi", [1, CH], BF16)
    Rn = [AR.alloc(f"Rn{i}", [128, CH], BF16) for i in range(3)]
    B_Rn = [Buf() for _ in range(3)]
    B_car = Buf()
    ZB, CBK, OB, QB = [0, 1, 2, 3], [4, 5], [6, 6], [7, 7]
    SC_SB = 128 ** -0.5

    def mask_idx(s, kb):
        if s == 0:
            return kb
        first = NKB[s] - 5
        if kb < first:
            return None
        return 4 + 5 * (s - 1) + (kb - first)

    def load_wq(h):
        if h < 8:
            P.dma("pool", wq[h % 2][:], w_in_v[:, :, h * 128:(h + 1) * 128], [], [B_wq[h % 2]], ch_wq[h % 2])

    qrot = [0]

    def qproj(h):
        if h >= 8:
            return
        for s in range(4):
            b = QB[qrot[0] % 2]
            qrot[0] += 1

            def mm(e):
                r = None
                for k in range(16):
                    r = e.matmul(PS(b, 0, CH), wq[h % 2][:, k, :], xT_own[:, k, s, 0:CH], start=(k == 0), stop=(k == 15))
                return r
            P.op("pe", mm, [B_wq[h % 2]] + [B_xTo[k][s] for k in range(16)], [PB[b]])
            P.op("dve", lambda e: e.tensor_scalar(qT[h % 2][:, s, 0:CH], PS(b, 0, CH), SC_SB, None, ALU.mult), [PB[b]], [B_qT[h % 2]])

    import os
    SKIP = os.environ.get("SB_SKIP", "")
    steps = []
    chain = 0
    for h in range(8):
        for s in range(4):
            for kb in range(NKB[s] - 1, -1, -1):
                steps.append((h, s, kb, chain))
            chain += 1

    def zbank(i):
        return 2 * ((i // 2) % 2) + (i % 2)

    def stA(i):
        h, s, kb, c = steps[i]
        zb = zbank(i)
        mi = mask_idx(s, kb)

        def mm(e):
            r = e.matmul(PS(zb, 0, CH), kT_sb[:, h, kb * 128:(kb + 1) * 128], qT[h % 2][:, s, 0:CH], start=True, stop=(mi is None))
            if mi is not None:
                r = e.matmul(PS(zb, 0, CH), cbf[:, IDB, :], msk[:, mi, 0:CH], start=False, stop=True)
            return r
        P.op("pe", mm, [B_big, B_qT[h % 2], B_msk, B_const], [PB[zb]])

    def stBp(u):
        p = u % 2
        w3, w2 = u % 3, u % 2
        zsrc = psA[:, 2 * p:2 * p + 2, 0:CH]
        P.op("act", lambda e: e.activation(EZ[w3][:, :, 0:CH], zsrc, AF.Exp), [PB[2 * p], PB[2 * p + 1]], [B_EZ[w3]])
        P.op("act", lambda e: e.activation(SPt[w2][:, :, 0:CH], EZ[w3][:, :, 0:CH], AF.Ln, bias=1.0), [B_EZ[w3]], [B_SP[w2]])

    def stC(i):
        h, s, kb, c = steps[i]
        z = i % 2
        u = i // 2
        first = (kb == NKB[s] - 1)
        last = (kb == 0)

        P.op("pe", lambda e: e.matmul(PS(CBK[z], 0, CH), cbf[:, NTRI, :], SPt[u % 2][:, i % 2, 0:CH], start=True, stop=first),
             [B_SP[u % 2], B_const], [PB[CBK[z]]])
        r3, p3 = i % 3, (i - 1) % 3
        sp = SPt[u % 2][:, i % 2, 0:CH]
        if not first:
            P.op("pe", lambda e: e.matmul(PS(CBK[z], 0, CH), cbf[:, ONES, :], Rn[p3][:], start=False, stop=True),
                 [B_Rn[p3], B_const], [PB[CBK[z]]])
        if not last:
            if first:
                P.op("dve", lambda e: e.tensor_scalar(Rn[r3][:], sp, -1.0, None, ALU.mult), [B_SP[u % 2]], [B_Rn[r3]])
            else:
                P.op("dve", lambda e: e.scalar_tensor_tensor(Rn[r3][:], sp, -1.0, Rn[p3][:], ALU.mult, ALU.add),
                     [B_SP[u % 2], B_Rn[p3]], [B_Rn[r3]])

    def stD(i):
        z = i % 2
        P.op("act", lambda e: e.activation(EC[z][:], PS(CBK[z], 0, CH), AF.Exp), [PB[CBK[z]]], [B_EC[z]])

    def stE(i):
        z = i % 2
        u = i // 2
        P.op("dve", lambda e: e.tensor_tensor(Am[z][:], EZ[u % 3][:, i % 2, 0:CH], EC[z][:], ALU.mult), [B_EZ[u % 3], B_EC[z]], [B_Am[z]])

    def stF(i):
        h, s, kb, c = steps[i]
        z = i % 2
        first = (kb == NKB[s] - 1)
        last = (kb == 0)
        ob = OB[0]
        P.op("pe", lambda e: e.matmul(PS(ob, 0, CH), v_sb[:, kb, h * 128:(h + 1) * 128], Am[z][:], start=first, stop=last),
             [B_big, B_Am[z]], [PB[ob]])
        if last:
            P.op("dve", lambda e: e.tensor_copy(oT_sb[:, h, s, 0:CH], PS(ob, 0, CH)), [PB[ob]], [B_oT])

    load_wq(0)
    load_wq(1)
    qproj(0)
    nst = len(steps)
    import os
    if os.environ.get("SB_LIMIT"):
        nst = int(os.environ["SB_LIMIT"])
    assert nst % 2 == 0
    for t in range(nst + 7):
        if t < nst:
            h, s, kb, c = steps[t]
            if s == 0 and kb == NKB[0] - 1:
                qproj(h + 1)
            if s == 1 and kb == NKB[1] - 1:
                load_wq(h + 2)
            stA(t)
        pe_warm(WB[0], v_sb[:, 0, 0:512])
        if t >= 2 and t % 2 == 0 and (t - 2) // 2 < nst // 2:
            stBp((t - 2) // 2)
        for lag, st in ((3, stC), (4, stD), (5, stE), (6, stF)):
            if 0 <= t - lag < nst:
                st(t - lag)
    dump("oTsb", oT_sb[:, 0, 0, 0:CH], [128, CH], [B_oT])
    P.barrier()
    AR.release("kT_sb", "v_sb", "xT_own", "wq0", "wq1", "qT0", "qT1", "msk", "EZ0", "EZ1", "SP0", "SP1",
               "EC0", "EC1", "Am0", "Am1", "EZ2", "EC2", "Am2", "chi", "Rn0", "Rn1", "Rn2")

    if stop_here("B"):
        return nc
    tmpf = [AR.alloc(f"tof{i}", [64, 4 * CHP], F32) for i in range(3)]
    tmpi = AR.alloc("toi", [64, 4 * CHP], I32)
    CCo = AR.alloc("CCo", [64, 4 * CHP], F32)
    SSo = AR.alloc("SSo", [64, 4 * CHP], F32)
    B_tabo = Buf()
    make_tables(bass.AP(pos_own, 0, [[0, 64], [1, 4 * CHP]]), 4 * CHP, CCo[:], SSo[:], tmpf, tmpi, B_tabo, chp)
    P.barrier()
    AR.release("tof0", "tof1", "tof2", "toi")
    oT_ml = AR.alloc("oT_ml", [128, 8, 4, CHP], F32)
    msk = AR.alloc("msk2", [128, 19, MW], BF16)
    P.dma("sp", msk[:], mml_d.ap(), [], [B_msk], chm)
    wuq = [AR.alloc(f"wuq{i}", [128, 4, 192], BF16) for i in range(2)]
    wuqs = [AR.alloc(f"wuqs{i}", [128, 4, 64], BF16) for i in range(2)]
    wukv = [AR.alloc(f"wukv{i}", [128, 2, 256], BF16) for i in range(2)]
    B_wm = [Buf() for _ in range(2)]
    ch_wm = [P.chan(f"wm{i}") for i in range(2)]
    qn = [AR.alloc(f"qn{i}", [128, 4, CHP], BF16) for i in range(2)]
    qr = [AR.alloc(f"qr{i}", [64, 4, CHP], BF16) for i in range(2)]
    kn = [AR.alloc(f"kn{i}", [128, S], BF16) for i in range(2)]
    vm = [AR.alloc(f"vm{i}", [128, 16, 128], BF16) for i in range(2)]
    B_hd = [Buf() for _ in range(2)]
    t1q = AR.alloc("t1q", [64, CH], F32)
    t2q = AR.alloc("t2q", [64, CH], F32)
    B_tq = Buf()
    Pm = [AR.alloc(f"Pm{i}", [128, CH], BF16) for i in range(4)]
    B_Pm = [Buf() for _ in range(4)]
    rd = AR.alloc("rd", [128, CH], F32)
    B_rd = Buf()
    w_uq_v = w_uq.ap().rearrange("(j p) c -> p j c", p=128)
    w_ukv_v = w_ukv.ap().rearrange("(j p) c -> p j c", p=128)
    SB_, OBm, DB, PBk = [0, 1], [2, 3], [4, 5], [6, 7]
    SC_ML = 192 ** -0.5

    def load_wm(h):
        if h >= 8:
            return
        i = h % 2
        P.dma("pool", wuq[i][:], w_uq_v[:, :, 192 * h:192 * h + 192], [], [B_wm[i]], ch_wm[i])
        P.dma("pool", wuqs[i][:, :, 0:32], w_uq_v[:, :, 192 * h + 160:192 * h + 192], [], [B_wm[i]], ch_wm[i])
        P.dma("pool", wuqs[i][:, :, 32:64], w_uq_v[:, :, 192 * h + 128:192 * h + 160], [], [B_wm[i]], ch_wm[i])
        P.dma("pool", wukv[i][:], w_ukv_v[:, :, 256 * h:256 * h + 256], [], [B_wm[i]], ch_wm[i])

    prot = [0]

    def pbank():
        b = PBk[prot[0] % 2]
        prot[0] += 1
        return b

    def mproj_units(h):
        units = []
        if h >= 8:
            return units
        i = h % 2

        def u_qn(s):
            b = pbank()

            def mm(e):
                r = None
                for j in range(4):
                    r = e.matmul(PS(b, 0, CH), wuq[i][:, j, 0:128], cqnT[:, j, s, 0:CH], start=(j == 0), stop=(j == 3))
                return r
            P.op("pe", mm, [B_wm[i], B_big], [PB[b]])
            evac_copy(qn[i][:, s, 0:CH], PS(b, 0, CH), [PB[b]], [B_hd[i]])

        def u_rope(s, which):
            wsel, c0, c1, tq, tab = ((wuq[i], 128, 192, t1q, CCo), (wuqs[i], 0, 64, t2q, SSo))[which]
            bb = pbank()

            def mm2(e):
                r = None
                for j in range(4):
                    r = e.matmul(PS(bb, 0, CH, 0, 64), wsel[:, j, c0:c1], cqnT[:, j, s, 0:CH], start=(j == 0), stop=(j == 3))
                return r
            P.op("pe", mm2, [B_wm[i], B_big], [PB[bb]])
            P.op("dve", lambda e: e.tensor_tensor(tq[:], PS(bb, 0, CH, 0, 64), tab[:, s * CHP:s * CHP + CH], ALU.mult),
                 [PB[bb], B_tabo], [B_tq])
            if which == 1:
                P.op("dve", lambda e: e.tensor_tensor(qr[i][:, s, 0:CH], t1q[:], t2q[:], ALU.add), [B_tq], [B_hd[i]])

        def u_kn(n):
            b = pbank()

            def mm(e):
                e.matmul(PS(b, 0, 512), wukv[i][:, 0, 0:128], ckvnT[:, 0, n * 512:(n + 1) * 512], start=True, stop=False)
                return e.matmul(PS(b, 0, 512), wukv[i][:, 1, 0:128], ckvnT[:, 1, n * 512:(n + 1) * 512], start=False, stop=True)
            P.op("pe", mm, [B_wm[i], B_big], [PB[b]])
            evac_copy(kn[i][:, n * 512:(n + 1) * 512], PS(b, 0, 512), [PB[b]], [B_hd[i]])

        def u_vm(k4):
            b = pbank()

            def mm(e):
                r = None
                for q in range(4):
                    kb = 4 * k4 + q
                    e.matmul(PS(b, q * 128, (q + 1) * 128), ckvnT[:, 0, kb * 128:(kb + 1) * 128], wukv[i][:, 0, 128:256], start=True, stop=False)
                    r = e.matmul(PS(b, q * 128, (q + 1) * 128), ckvnT[:, 1, kb * 128:(kb + 1) * 128], wukv[i][:, 1, 128:256], start=False, stop=True)
                return r
            P.op("pe", mm, [B_wm[i], B_big], [PB[b]])
            evac_copy(vm[i][:, 4 * k4:4 * k4 + 4, :], PSV(b, [[128, 4], [1, 128]]), [PB[b]], [B_hd[i]])

        for s in range(4):
            units.append(lambda s=s: u_qn(s))
            units.append(lambda s=s: u_rope(s, 0))
            units.append(lambda s=s: u_rope(s, 1))
        for n in range(4):
            units.append(lambda n=n: u_kn(n))
        for k4 in range(4):
            units.append(lambda k4=k4: u_vm(k4))
        return units

    msteps = []
    chain = 0
    for h in range(8):
        for s in range(4):
            for kb in range(NKB[s]):
                msteps.append((h, s, kb, chain))
            chain += 1

    def mlA(i):
        h, s, kb, c = msteps[i]
        z = i % 2
        hi = h % 2
        mi = mask_idx(s, kb)

        def mm(e):
            e.matmul(PS(SB_[z], 0, CH), kn[hi][:, kb * 128:(kb + 1) * 128], qn[hi][:, s, 0:CH], start=True, stop=False)
            r = e.matmul(PS(SB_[z], 0, CH), krT[:, kb * 128:(kb + 1) * 128], qr[hi][:, s, 0:CH], start=False, stop=(mi is None))
            if mi is not None:
                r = e.matmul(PS(SB_[z], 0, CH), cbf[:, IDB, :], msk[:, mi, 0:CH], start=False, stop=True)
            return r
        P.op("pe", mm, [B_hd[hi], B_big, B_msk, B_const], [PB[SB_[z]]])

    def mlB(i):
        z, w = i % 2, i % 4
        P.op("act", lambda e: e.activation(Pm[w][:], PS(SB_[z], 0, CH), AF.Exp, scale=SC_ML), [PB[SB_[z]]], [B_Pm[w]])

    def mlC(i):
        h, s, kb, c = msteps[i]
        w = i % 4
        hi = h % 2
        first = (kb == 0)
        last = (kb == NKB[s] - 1)
        ob, db = OBm[c % 2], DB[c % 2]

        def mm(e):
            e.matmul(PS(ob, 0, CH), vm[hi][:, kb, :], Pm[w][:], start=first, stop=last)
            return e.matmul(PS(db, 0, CH), cbf[:, ONES, :], Pm[w][:], start=first, stop=last)
        P.op("pe", mm, [B_hd[hi], B_Pm[w], B_const], [PB[ob], PB[db]])
        if last:
            P.op("act", lambda e: e.activation(rd[:], PS(db, 0, CH), AF.Ln), [PB[db]], [B_rd])
            P.op("act", lambda e: e.activation(rd[:], rd[:], AF.Exp, scale=-1.0), [B_rd], [B_rd])
            P.op("dve", lambda e: e.tensor_tensor(oT_ml[:, h, s, 0:CH], PS(ob, 0, CH), rd[:], ALU.mult), [PB[ob], B_rd], [B_oT])

    load_wm(0)
    load_wm(1)
    for u in mproj_units(0):
        u()
    nst = len(msteps)
    pend_units = []
    hold = 0
    for t in range(nst + 3):
        if t < nst:
            h, s, kb, c = msteps[t]
            if s == 0 and kb == 0:
                pend_units = mproj_units(h + 1)
                hold = 4
            if s == 2 and kb == 0:
                load_wm(h + 2)
            mlA(t)
        if 0 <= t - 1 < nst:
            mlB(t - 1)
        if 0 <= t - 3 < nst:
            mlC(t - 3)
        if hold > 0:
            hold -= 1
        elif pend_units:
            pend_units.pop(0)()
    dump("oTml", oT_ml[:, 0, 0, 0:CH], [128, CH], [B_oT])
    P.barrier()
    AR.release("cqnT", "ckvnT", "krT", "msk2", "CCo", "SSo", "wuq0", "wuq1",
               "wuqs0", "wuqs1", "wukv0", "wukv1", "qn0", "qn1", "qr0", "qr1", "kn0", "kn1", "vm0", "vm1",
               "t1q", "t2q", "Pm0", "Pm1", "Pm2", "Pm3", "rd")

    if stop_here("C"):
        return nc
    wo = AR.alloc("wo", [128, 16, D], BF16)
    B_woq = [Buf() for _ in range(4)]
    w_o_v = w_o.ap().rearrange("(k p) c -> p k c", p=128)
    for q in range(4):
        P.dma("pool", wo[:, 4 * q:4 * q + 4, :], w_o_v[:, 4 * q:4 * q + 4, :], [], [B_woq[q]], P.chan(f"wo{q}"))
    B_wo = B_woq[0]
    mixT = AR.alloc("mixT", [128, 16, 1024], BF16)
    mixH = AR.alloc("mixH", [128, 16, 128], BF16)
    B_mix = Buf()
    P.op("dve", lambda e: e.memset(mixH[:], 0.0), [], [B_mix])
    sqd = [AR.alloc(f"sqd{i}", [128, CH], BF16) for i in range(2)]
    B_sqd = [Buf() for _ in range(2)]
    rbd = [AR.alloc(f"rbd{i}", [128, CH], F32) for i in range(2)]
    B_rbd = [Buf() for _ in range(2)]
    it = 0
    for s in range(4):
        for grp, oT, gcol in ((0, oT_sb, GOSB), (1, oT_ml, GOMLA)):
            bss = it % 2
            rb, Brb = rbd[it % 2], B_rbd[it % 2]
            it += 1
            for h in range(8):
                z = h % 2
                P.op("act", lambda e: e.activation(sqd[z][:], oT[:, h, s, 0:CH], AF.Square), [B_oT], [B_sqd[z]])
                P.op("pe", lambda e: e.matmul(PS(bss, 0, CH), cbf[:, ONES, :], sqd[z][:], start=(h == 0), stop=(h == 7)),
                     [B_sqd[z], B_const], [PB[bss]])
            rsqrt_ops(rb[:], PS(bss, 0, CH), 1.0 / 1024, [PB[bss]], [Brb])
            for h in range(8):
                f = grp * 8 + h
                P.op("dve", lambda e: e.scalar_tensor_tensor(mixT[:, f, s * 256:(s + 1) * 256], oT[:, h, s, 2:CH],
                                                             gp[:, gcol + h:gcol + h + 1], rb[:, 2:CH], ALU.mult, ALU.mult),
                     [B_oT, Brb, B_const], [B_mix])
                P.op("dve", lambda e: e.scalar_tensor_tensor(mixH[:, f, 32 * s:32 * s + 2], oT[:, h, s, 0:2],
                                                             gp[:, gcol + h:gcol + h + 1], rb[:, 0:2], ALU.mult, ALU.mult),
                     [B_oT, Brb, B_const], [B_mix])
    P.barrier()
    AR.release("oT_sb", "oT_ml", "sqd0", "sqd1", "rbd0", "rbd1")

    if stop_here("D1"):
        return nc
    h2T = AR.alloc("h2T", [128, 16, 4, CHP], BF16)
    B_h2 = Buf()
    gpost_b = AR.alloc("gpost_b", [128, D], F32)
    gffn_b = AR.alloc("gffn_b", [128, D], F32)
    P.dma("sp", gpost_b[:], bass.AP(grow, D, [[0, 128], [1, D]]), [], [B_gb], chg)
    P.dma("sp", gffn_b[:], bass.AP(grow, 2 * D, [[0, 128], [1, D]]), [], [B_gb], chg)
    xr = [AR.alloc(f"xr{i}", [128, D], F32) for i in range(2)]
    B_xr = [Buf() for _ in range(2)]
    ch_xr = [P.chan(f"xr{i}") for i in range(2)]
    xnb = [AR.alloc(f"xnb{i}", [128, D], BF16) for i in range(2)]
    B_xnb = [Buf() for _ in range(2)]
    t1 = [AR.alloc(f"t1_{i}", [128, D], F32) for i in range(2)]
    B_t1 = [Buf() for _ in range(2)]
    ch_x1 = [P.chan(f"x1w{i}") for i in range(2)]
    B_x1s = [Buf() for _ in range(8)]
    wu = [[AR.alloc(f"wu0_{hh}", [128, 16, 128], BF16) for hh in range(2)]]
    B_wu = [Buf() for _ in range(3)]
    ch_wu = [P.chan(f"wu{i}") for i in range(3)]
    w_up_v = w_up.ap().rearrange("(k p) c -> p k c", p=128)

    def load_wu(ct):
        if ct < NCT:
            i = ct % 3
            P.dma("pool", wu[i][0][:], w_up_v[:, :, ct * 128:(ct + 1) * 128], [], [B_wu[i]], ch_wu[i])
            P.dma("pool", wu[i][1][:], w_up_v[:, :, DFF + ct * 128:DFF + (ct + 1) * 128], [], [B_wu[i]], ch_wu[i])

    load_wu(0)
    B_h2l = []

    def d2_stageA(tb):
        z = tb % 2
        src = x_halo.ap() if tb == 8 else x_own.ap()[tb * 128:(tb + 1) * 128, :]
        P.dma("sp", xr[z][:], src, [], [B_xr[z]], ch_xr[z])
        for n in range(4):
            def mm(e):
                r = None
                for f in range(16):
                    lhs = mixH[:, f, :] if tb == 8 else mixT[:, f, tb * 128:(tb + 1) * 128]
                    r = e.matmul(PS(n, 0, 512), lhs, wo[:, f, n * 512:(n + 1) * 512], start=(f == 0), stop=(f == 15))
                return r
            P.op("pe", mm, [B_mix] + B_woq, [PB[n]])

    def d2_stageAev(tb):
        z = tb % 2
        c0 = 32 + tb
        cp = 64 + 4 * tb
        for n in range(4):
            P.op("act", lambda e: e.activation(xnb[z][:, n * 512:(n + 1) * 512], PS(n, 0, 512), AF.Square,
                                               accum_out=ssA[:, cp + n:cp + n + 1]), [PB[n]], [B_xnb[z], B_ss[c0]])
            P.op("dve", lambda e: e.tensor_tensor(t1[z][:, n * 512:(n + 1) * 512], PS(n, 0, 512),
                                                  gpost_b[:, n * 512:(n + 1) * 512], ALU.mult), [PB[n], B_gb], [B_t1[z]])
        P.op("dve", lambda e: e.tensor_reduce(ssA[:, c0:c0 + 1], ssA[:, cp:cp + 4], mybir.AxisListType.X, ALU.add),
             [B_ss[c0]], [B_ss[c0]])

    def d2_stageB(tb):
        z = tb % 2
        c0 = 32 + tb
        c1 = 41 + tb
        rsqrt_ops(rsA[:, c0:c0 + 1], ssA[:, c0:c0 + 1], 1.0 / D, [B_ss[c0]], [B_ss[c0]])
        P.op("dve", lambda e: e.scalar_tensor_tensor(t1[z][:], t1[z][:], rsA[:, c0:c0 + 1], xr[z][:], ALU.mult, ALU.add),
             [B_t1[z], B_ss[c0], B_xr[z]], [B_t1[z]])
        if tb < 8:
            P.dma("sp", x1s_d.ap()[tb * 128:(tb + 1) * 128, :], t1[z][:], [B_t1[z]], [B_x1s[tb]], ch_x1[z])
        if tb == 0:
            dump("x1", t1[z][:, 0:256], [128, 256], [B_t1[z]])
        P.op("act", lambda e: e.activation(xnb[z][:], t1[z][:], AF.Square, accum_out=ssA[:, c1:c1 + 1]), [B_t1[z]], [B_xnb[z], B_ss[c1]])
        rsqrt_ops(rsA[:, c1:c1 + 1], ssA[:, c1:c1 + 1], 1.0 / D, [B_ss[c1]], [B_ss[c1]])
        P.op("dve", lambda e: e.scalar_tensor_tensor(xnb[z][:], t1[z][:], rsA[:, c1:c1 + 1], gffn_b[:], ALU.mult, ALU.mult),
             [B_t1[z], B_ss[c1], B_gb], [B_xnb[z]])
        for q in range(4):
            b = 4 + q

            def tr(e):
                r = None
                for kk in range(4):
                    k = 4 * q + kk
                    r = e.transpose(PSB(b)[:, kk * 128:(kk + 1) * 128], xnb[z][:, k * 128:(k + 1) * 128], cbf[:, IDB, :])
                return r
            P.op("pe", tr, [B_xnb[z], B_const], [PB[b]])
            if tb < 8:
                s_, off = tb // 2, 2 + 128 * (tb % 2)
                o_ap = h2T[:, 4 * q:4 * q + 4, s_, off:off + 128]
                i_ap = PSB(b)[:, 0:512].rearrange("p (a b) -> p a b", b=128)
            else:
                o_ap = h2T[:, 4 * q:4 * q + 4, :, 0:2]
                i_ap = PSB(b)[:, 0:512].rearrange("p (k s j) -> p k s j", k=4, s=4, j=32)[:, :, :, 0:2]
            bh = Buf()
            B_h2l.append(bh)
            evac_copy(o_ap, i_ap, [PB[b]], [bh])

    d2_stageA(0)
    d2_stageAev(0)
    for tb in range(9):
        if tb + 1 < 9:
            d2_stageA(tb + 1)
        d2_stageB(tb)
        if tb + 1 < 9:
            d2_stageAev(tb + 1)
    hfv = bass.AP(hf, 0, [[8, 128], [0, 16], [2, 4], [1, 2]])
    P.op("dve", lambda e: e.tensor_tensor(h2T[:, :, :, 0:2], h2T[:, :, :, 0:2], hfv, ALU.mult), B_h2l + [B_const], B_h2l)
    P.barrier()
    AR.release("wo", "mixT", "mixH", "gpost_b", "gffn_b", "xr0", "xr1", "t1_0", "t1_1", "xnb0", "xnb1")

    if stop_here("D2"):
        return nc
    G = AR.alloc("G", [128, NCT, 1024], BF16, top=True)
    B_G = Buf()
    wu.append([AR.alloc(f"wu1_{hh}", [128, 16, 128], BF16) for hh in range(2)])
    wu.append([AR.alloc(f"wu2_{hh}", [128, 16, 128], BF16) for hh in range(2)])
    load_wu(1)
    A0 = [AR.alloc(f"A0_{i}", [128, 4, CHP], F32) for i in range(2)]
    A1 = [AR.alloc(f"A1_{i}", [128, 4, 256], F32) for i in range(2)]
    Y = [AR.alloc(f"Y_{i}", [128, 4, 256], F32) for i in range(2)]
    Gg = AR.alloc("Gg", [128, 4, 256], F32)
    B_A0 = [Buf() for _ in range(2)]
    B_A1 = [Buf() for _ in range(2)]
    B_Y = [Buf() for _ in range(2)]
    B_Gg = Buf()
    for ct in range(NCT):
        load_wu(ct + 2)
        i = ct % 3
        for hh in range(2):
            base = 4 * hh
            for s in range(4):
                def mm(e):
                    r = None
                    for k in range(16):
                        r = e.matmul(PS(base + s, 0, CH), wu[i][hh][:, k, :], h2T[:, k, s, 0:CH], start=(k == 0), stop=(k == 15))
                    return r
                P.op("pe", mm, [B_wu[i], B_h2], [PB[base + s]])
            U = PST(base)
            col = hh * NCT + ct
            pbs = PB[base:base + 4]
            P.op("act", lambda e: e.activation(A0[hh][:, :, 0:CH], U[:, :, 0:CH], AF.Identity, scale=gp[:, CW2 + col:CW2 + col + 1],
                                               bias=gp[:, CB + col:CB + col + 1]), pbs + [B_const], [B_A0[hh]])
            P.op("dve", lambda e: e.scalar_tensor_tensor(A1[hh][:], U[:, :, 1:257], gp[:, CW1 + col:CW1 + col + 1],
                                                         A0[hh][:, :, 2:CH], ALU.mult, ALU.add), pbs + [B_A0[hh], B_const], [B_A1[hh]])
            P.op("dve", lambda e: e.scalar_tensor_tensor(Y[hh][:], U[:, :, 0:256], gp[:, CW0 + col:CW0 + col + 1],
                                                         A1[hh][:], ALU.mult, ALU.add), pbs + [B_A1[hh], B_const], [B_Y[hh]])
        P.op("act", lambda e: e.activation(Gg[:], Y[0][:], AF.Gelu_apprx_tanh), [B_Y[0]], [B_Gg])
        P.op("dve", lambda e: e.tensor_tensor(G[:, ct, :].rearrange("p (a b) -> p a b", b=256), Gg[:], Y[1][:], ALU.mult),
             [B_Gg, B_Y[1]], [B_G])
    dump("G0", G[:, 0, 0:256], [128, 256], [B_G])
    wd = [AR.alloc("wd0", [128, NCT, 128], BF16, top=True)]
    B_wd = [Buf() for _ in range(2)]
    ch_wd = [P.chan(f"wd{i}") for i in range(2)]
    w_dn_v = w_down.ap().rearrange("(c p) n -> p c n", p=128)

    def load_wd(m):
        if m < 16:
            i = m % 2
            P.dma("pool", wd[i][:, 0:22, :], w_dn_v[:, 0:22, m * 128:(m + 1) * 128], [], [B_wd[i]], ch_wd[i])
            P.dma("pool", wd[i][:, 22:44, :], w_dn_v[:, 22:44, m * 128:(m + 1) * 128], [], [B_wd[i]], ch_wd[i])

    load_wd(0)
    P.barrier()
    AR.release("h2T", "A0_0", "A0_1", "A1_0", "A1_1", "Y_0", "Y_1", "Gg",
               "wu0_0", "wu0_1", "wu1_0", "wu1_1", "wu2_0", "wu2_1")

    if stop_here("E"):
        return nc
    y2T = AR.alloc("y2T", [128, 16, 1024], F32)
    B_y2 = [Buf() for _ in range(16)]
    wd.append(AR.alloc("wd1", [128, NCT, 128], BF16))
    sqF = [AR.alloc(f"sqF{i}", [128, 256], BF16) for i in range(2)]
    B_sqF = [Buf() for _ in range(2)]
    rbF = AR.alloc("rbF", [128, 1024], F32)
    B_rbF = Buf()
    def ssmm(m, s, z):
        P.op("pe", lambda e: e.matmul(PS(4 + s, 0, 256), cbf[:, ONES, :], sqF[z][:], start=(m == 0), stop=(m == 15)),
             [B_sqF[z], B_const], [PB[4 + s]])

    pendF = [None]
    it = 0
    for m in range(16):
        load_wd(m + 1)
        i = m % 2
        for s in range(4):
            b = s
            z = it % 2
            it += 1

            def mm(e):
                r = None
                for c in range(NCT):
                    r = e.matmul(PS(b, 0, 256), wd[i][:, c, :], G[:, c, s * 256:(s + 1) * 256], start=(c == 0), stop=(c == NCT - 1))
                return r
            P.op("pe", mm, [B_wd[i], B_G], [PB[b]])
            P.op("act", lambda e: e.activation(y2T[:, m, s * 256:(s + 1) * 256], PS(b, 0, 256), AF.Copy), [PB[b]], [B_y2[m]])
            P.op("act", lambda e: e.activation(sqF[z][:], PS(b, 0, 256), AF.Square), [PB[b]], [B_sqF[z]])
            if pendF[0] is not None:
                ssmm(*pendF[0])
            pendF[0] = (m, s, z)
    ssmm(*pendF[0])
    for s in range(4):
        rsqrt_ops(rbF[:, s * 256:(s + 1) * 256], PS(4 + s, 0, 256), 1.0 / D, [PB[4 + s]], [B_rbF])
    P.barrier()
    AR.release("G", "wd0", "wd1", "sqF0", "sqF1")

    if stop_here("F"):
        return nc
    x1r = [AR.alloc(f"x1r{i}", [128, D], F32) for i in range(2)]
    B_x1r = [Buf() for _ in range(2)]
    ch_x1r = [P.chan(f"x1r{i}") for i in range(2)]
    ot = [AR.alloc(f"ot{i}", [128, D], F32) for i in range(2)]
    B_ot = [Buf() for _ in range(2)]
    ch_ot = [P.chan(f"ot{i}") for i in range(2)]
    for m in range(16):
        P.op("dve", lambda e: e.scalar_tensor_tensor(y2T[:, m, :], y2T[:, m, :], gp[:, GFP + m:GFP + m + 1], rbF[:], ALU.mult, ALU.mult),
             [B_y2[m], B_rbF, B_const], [B_y2[m]])
    outs = []
    for tb in range(8):
        z = tb % 2
        base = 4 * z
        P.dma("sp", x1r[z][:], x1s_d.ap()[tb * 128:(tb + 1) * 128, :], [B_x1s[tb]], [B_x1r[z]], ch_x1r[z])
        for q in range(4):
            b = base + q

            def tr(e):
                r = None
                for mm_ in range(4):
                    m = 4 * q + mm_
                    r = e.transpose(PS(b, mm_ * 128, (mm_ + 1) * 128), y2T[:, m, tb * 128:(tb + 1) * 128], ident[:])
                return r
            P.op("pe", tr, [B_y2[4 * q + mm_] for mm_ in range(4)] + [B_const], [PB[b]])
        yv = PST(base)[:, :, :]
        P.op("dve", lambda e: e.tensor_tensor(ot[z][:].rearrange("p (a b) -> p a b", b=512), yv,
                                              x1r[z][:].rearrange("p (a b) -> p a b", b=512), ALU.add),
             PB[base:base + 4] + [B_x1r[z]], [B_ot[z]])
        outs.append(P.dma("sp", out_d.ap()[tb * 128:(tb + 1) * 128, :], ot[z][:], [B_ot[z]], [], ch_ot[z]))
    P.finish(outs + list(dbg_outs.values()))
    return nc


_NC_CACHE = {}
_PREP_ONLY = False


def _const_inputs():
    ident = np.eye(128, dtype=np.float32)
    cbf = np.zeros((128, 4, 128), dtype=np.float32)
    cbf[:, 0, :] = np.eye(128)
    cbf[:, 1, :] = 1.0
    j = np.arange(128)[:, None]
    k = np.arange(128)[None, :]
    cbf[:, 2, :] = np.where(j >= k, -1.0, 0.0)
    cbf[0, 3, :] = 1.0
    return ident, cbf.astype(ml_dtypes.bfloat16)


def _masks(r):
    msb = np.zeros((128, 19, MW), dtype=np.float32)
    mml = np.zeros((128, 19, MW), dtype=np.float32)
    col = np.arange(CH)
    for s in range(4):
        ch = CHUNKS[r][s]
        qpos = np.where(col < 2, 256 * ch - 2 + col, 256 * ch + col - 2)
        valid = qpos >= 0
        kbs = [0, 1, 2, 3] if s == 0 else list(range(NKB[s] - 5, NKB[s]))
        for kb in kbs:
            ti = kb if s == 0 else 4 + 5 * (s - 1) + (kb - (NKB[s] - 5))
            kpos = 128 * kb + np.arange(128)
            m1 = (kpos[:, None] >= qpos[None, :]) | (~valid)[None, :]
            m2 = ((kpos[:, None] // 64) > (qpos[None, :] // 64)) & valid[None, :]
            msb[:, ti, 0:CH] = np.where(m1, NEG, 0.0)
            mml[:, ti, 0:CH] = np.where(m2, NEG, 0.0)
    return msb.astype(ml_dtypes.bfloat16), mml.astype(ml_dtypes.bfloat16)


def kernel(x, positions, g_attn_pre, w_in, g_cq, w_uq, g_ckv, w_ukv, g_out_sb, g_out_mla, w_o, g_attn_post,
           g_ffn_pre, w_up, conv_w, conv_b, w_down, g_ffn_post):
    x = np.asarray(x, dtype=np.float32)
    positions = np.asarray(positions, dtype=np.int32)
    f = lambda a: np.ascontiguousarray(np.asarray(a, dtype=np.float32))
    ident, cbf = _const_inputs()
    gp = np.zeros((128, NG), dtype=np.float32)
    gp[:, GCQ:GCQ + 4] = f(g_cq)[0].reshape(4, 128).T
    gp[:, GCKV:GCKV + 2] = f(g_ckv)[0].reshape(2, 128).T
    gp[:, GOSB:GOSB + 8] = f(g_out_sb)[0].reshape(8, 128).T
    gp[:, GOMLA:GOMLA + 8] = f(g_out_mla)[0].reshape(8, 128).T
    gp[:, GFP:GFP + 16] = f(g_ffn_post)[0].reshape(16, 128).T
    gp[:, CB:CB + 88] = f(conv_b)[0].reshape(88, 128).T
    cw = f(conv_w)[0]
    gp[:, CW0:CW0 + 88] = cw[0].reshape(88, 128).T
    gp[:, CW1:CW1 + 88] = cw[1].reshape(88, 128).T
    gp[:, CW2:CW2 + 88] = cw[2].reshape(88, 128).T
    half = 32
    inv_freq = (np.float32(10000.0) ** (-np.arange(half, dtype=np.float32) / np.float32(half))).astype(np.float32)
    gp[0:32, INVF] = inv_freq
    gp[32:64, INVF] = inv_freq
    gp[0:32, SGN] = -1.0
    gp[32:64, SGN] = 1.0
    grow = np.stack([f(g_attn_pre)[0], f(g_attn_post)[0], f(g_ffn_pre)[0]], 0)
    weights = dict(w_in=f(w_in)[0], w_uq=f(w_uq)[0], w_ukv=f(w_ukv)[0], w_o=f(w_o)[0], w_up=f(w_up)[0], w_down=f(w_down)[0])
    masks = [_masks(0), _masks(1)]
    in_maps = []
    own_rows = []
    for core in range(8):
        b, r = core // 2, core % 2
        chunks = CHUNKS[r]
        idx = np.concatenate([np.arange(256 * ch, 256 * ch + 256) for ch in chunks])
        own_rows.append(idx)
        x_halo = np.zeros((128, D), dtype=np.float32)
        pos_own = np.zeros((1, 4 * CHP), dtype=np.int32)
        hflag = np.zeros((128, 8), dtype=np.float32)
        for s, ch in enumerate(chunks):
            pos_own[0, s * CHP + 2:s * CHP + CH] = positions[b, 256 * ch:256 * ch + 256]
            if ch > 0:
                x_halo[32 * s:32 * s + 2] = x[b, 256 * ch - 2:256 * ch]
                pos_own[0, s * CHP:s * CHP + 2] = positions[b, 256 * ch - 2:256 * ch]
                hflag[:, 2 * s:2 * s + 2] = 1.0
        m = dict(x_ctx=np.ascontiguousarray(x[b]), x_own=np.ascontiguousarray(x[b][idx]), x_halo=x_halo,
                 pos_ctx=np.ascontiguousarray(positions[b][None, :]), pos_own=pos_own, gp=gp, grow=grow,
                 ident=ident, cbf=cbf, msb=masks[r][0], mml=masks[r][1], hflag=hflag)
        m.update(weights)
        in_maps.append(m)
    if _PREP_ONLY:
        return in_maps, own_rows
    if "nc" not in _NC_CACHE:
        _NC_CACHE["nc"] = build_program()
    res = run_bass_kernel_spmd(_NC_CACHE["nc"], in_maps, core_ids=list(range(8)))
    out = np.zeros((NB, S, D), dtype=np.float32)
    for core in range(8):
        b = core // 2
        out[b, own_rows[core]] = res.results[core]["out"]
    if DEBUG:
        kernel.last_results = res.results
    return out
```
